# Optimizing a Trainium2 kernel written in Bass

```python
import math
import jax, jax.numpy as jnp
from jax import lax
import numpy as np

D_MODEL = 2048
BATCH = 2
SEQ = 4096
DEPTH = 1

GRID_W = 64
CTX_LEN = 256
EPS = 1e-6
RWKV_HEADS = 16
RWKV_HEAD = 64
RWKV_DIM = RWKV_HEADS * RWKV_HEAD
DECAY_LORA = 96
AAA_LORA = 96
GATE_LORA = 256
N_DIR = 2
DECAY_SCALE = math.exp(-0.5)
LNX_EPS = 64e-5
MLA_HEADS = 16
Q_LORA = 512
KV_LORA = 512
QK_NOPE = 128
QK_ROPE = 64
V_HEAD = 128
ROPE_THETA = 10000.0
ATTN_SCALE = (QK_NOPE + QK_ROPE) ** -0.5
Q_BLOCK = 128
D_FF = 5632
CONV_W = 3
N_BRANCH = 2
RWKV_IN = 3 * RWKV_DIM + N_DIR * DECAY_LORA + N_DIR * AAA_LORA + GATE_LORA
MLA_IN = Q_LORA + KV_LORA + QK_ROPE
GATE_IN = N_BRANCH * D_MODEL
D_IN = RWKV_IN + MLA_IN + GATE_IN
RWKV_SPLITS = [RWKV_DIM, 2 * RWKV_DIM, 3 * RWKV_DIM,
               3 * RWKV_DIM + N_DIR * DECAY_LORA,
               3 * RWKV_DIM + N_DIR * (DECAY_LORA + AAA_LORA)]

kernel_name = "hybrid_rwkv7_mla_convffn_dit_block"


def rms_norm(x, g, eps=EPS):
    xf = x.astype(jnp.float32)
    y = xf * lax.rsqrt(jnp.mean(xf * xf, axis=-1, keepdims=True) + eps)
    return (y * g.astype(jnp.float32)).astype(x.dtype)


def modulate(h, shift, scale):
    return h * (1.0 + scale) + shift


def centred_shift(x):
    xp = jnp.pad(x, ((0, 0), (1, 1), (0, 0)))
    return 0.5 * (xp[:, :-2] + xp[:, 2:])


def depthwise_conv(x, w, b):
    T = x.shape[1]
    pad = CONV_W // 2
    xp = jnp.pad(x, ((0, 0), (pad, pad), (0, 0)))
    y = b
    for j in range(CONV_W):
        y = y + xp[:, j:j + T] * w[j]
    return y


def axial_angles(n_tok):
    rows = n_tok // GRID_W
    row = jnp.repeat(jnp.arange(rows), GRID_W).astype(jnp.float32)
    col = jnp.tile(jnp.arange(GRID_W), rows).astype(jnp.float32)
    half = QK_ROPE // 2
    freqs = ROPE_THETA ** (-jnp.arange(0, half, 2, dtype=jnp.float32) / half)
    return row[:, None] * freqs, col[:, None] * freqs


def rotate(x, ang):
    x1, x2 = jnp.split(x, 2, axis=-1)
    cs, sn = jnp.cos(ang).astype(x.dtype), jnp.sin(ang).astype(x.dtype)
    return jnp.concatenate([x1 * cs - x2 * sn, x1 * sn + x2 * cs], axis=-1)


def rope_2d(x, ang_row, ang_col):
    xr, xc = jnp.split(x, 2, axis=-1)
    return jnp.concatenate([rotate(xr, ang_row), rotate(xc, ang_col)], axis=-1)


def wkv7_scan(r, w, k, v, kk, a, s0, reverse, emit):
    def step(S, inp):
        r_t, w_t, k_t, v_t, kk_t, a_t = inp
        sa = jnp.einsum('bhvk,bhk->bhv', S, -kk_t)
        S = (S * w_t[:, :, None, :] + sa[..., None] * (kk_t * a_t)[:, :, None, :]
             + v_t[..., None] * k_t[:, :, None, :])
        y = jnp.einsum('bhvk,bhk->bhv', S, r_t) if emit else None
        return S, y
    xs = tuple(jnp.moveaxis(t, 1, 0) for t in (r, w, k, v, kk, a))
    s_fin, ys = lax.scan(step, s0, xs, reverse=reverse)
    return (jnp.moveaxis(ys, 0, 1) if emit else None), s_fin


def rwkv_branch(p, s0, P, emit):
    B, T, _ = p.shape
    f32 = jnp.float32
    heads = lambda t: t.reshape(B, T, RWKV_HEADS, RWKV_HEAD).astype(f32)
    p = p + P['rwkv_mu'] * (centred_shift(p) - p)
    r, k, v, wl, al, gl = jnp.split(p, RWKV_SPLITS, axis=-1)
    wl = wl.reshape(B, T, N_DIR, DECAY_LORA)
    al = al.reshape(B, T, N_DIR, AAA_LORA)
    w_raw = P['rwkv_w0'] + jnp.einsum('btdr,drc->btdc', jnp.tanh(wl), P['rwkv_w2'])
    decay = jnp.exp(-DECAY_SCALE * jax.nn.sigmoid(w_raw.astype(f32)))
    a = jax.nn.sigmoid((P['rwkv_a0'] + jnp.einsum('btdr,drc->btdc', al, P['rwkv_a2'])).astype(f32))
    kk = heads(k * P['rwkv_k_k'])
    kk = kk / jnp.maximum(jnp.sqrt(jnp.sum(kk * kk, axis=-1, keepdims=True)), 1e-12)
    k_dir = k[:, :, None, :].astype(f32) * (1.0 + (a - 1.0) * P['rwkv_k_a'].astype(f32))
    r_h, v_h = heads(r), heads(v)
    ys, states = [], []
    for d in range(N_DIR):
        y_d, s_d = wkv7_scan(r_h, heads(decay[:, :, d]), heads(k_dir[:, :, d]), v_h, kk,
                             heads(a[:, :, d]), s0[d], reverse=(d == 1), emit=emit)
        ys.append(y_d)
        states.append(s_d)
    s_fin = jnp.stack(states)
    if not emit:
        return None, s_fin
    y = ys[0] + ys[1]
    mu = jnp.mean(y, axis=-1, keepdims=True)
    var = jnp.mean(jnp.square(y - mu), axis=-1, keepdims=True)
    yn = ((y - mu) * lax.rsqrt(var + LNX_EPS)).reshape(B, T, RWKV_DIM)
    yn = yn * P['rwkv_lnx_w'] + P['rwkv_lnx_b']
    k_bar = heads(0.5 * (k_dir[:, :, 0] + k_dir[:, :, 1]))
    bonus = (jnp.sum(r_h * k_bar * P['rwkv_r_k'].astype(f32), axis=-1, keepdims=True) * v_h).reshape(B, T, RWKV_DIM)
    g = jax.nn.sigmoid(gl) @ P['rwkv_g2']
    return ((yn + bonus).astype(p.dtype) * g), s_fin


def mla_kv(p_mla, P, ang):
    B, T, _ = p_mla.shape
    ckv = rms_norm(p_mla[..., Q_LORA:Q_LORA + KV_LORA], P['mla_kv_norm'])
    kpe = p_mla[..., Q_LORA + KV_LORA:]
    kv = (ckv @ P['mla_w_ukv']).reshape(B, T, MLA_HEADS, QK_NOPE + V_HEAD)
    k_nope, v = kv[..., :QK_NOPE], kv[..., QK_NOPE:]
    if ang is not None:
        kpe = rope_2d(kpe, ang[0], ang[1])
    k = jnp.concatenate([k_nope, jnp.broadcast_to(kpe[:, :, None, :], (B, T, MLA_HEADS, QK_ROPE))], axis=-1)
    return k, v


def mla_q(p_mla, P, ang):
    B, T, _ = p_mla.shape
    cq = rms_norm(p_mla[..., :Q_LORA], P['mla_q_norm'])
    q = (cq @ P['mla_w_uq']).reshape(B, T, MLA_HEADS, QK_NOPE + QK_ROPE)
    if ang is not None:
        q = jnp.concatenate([q[..., :QK_NOPE], rope_2d(q[..., QK_NOPE:], ang[0][:, None], ang[1][:, None])], axis=-1)
    return q


def softmax_attend(q, k, v):
    s = jnp.einsum('bqhd,bkhd->bhqk', q, k).astype(jnp.float32) * ATTN_SCALE
    pr = jax.nn.softmax(s, axis=-1).astype(v.dtype)
    return jnp.einsum('bhqk,bkhd->bqhd', pr, v)


def blocked_attention(q, k, v):
    B, T, H, Dk = q.shape
    nb = T // Q_BLOCK
    qb = jnp.moveaxis(q.reshape(B, nb, Q_BLOCK, H, Dk), 1, 0)
    ob = lax.map(lambda qi: softmax_attend(qi, k, v), qb)
    return jnp.moveaxis(ob, 0, 1).reshape(B, T, H, v.shape[-1])


def token_mixer(h, s0, ang, kv_prefix, P, emit):
    B, T, _ = h.shape
    p = h @ P['w_in']
    p_rwkv, p_mla, p_gate = jnp.split(p, [RWKV_IN, RWKV_IN + MLA_IN], axis=-1)
    y_rwkv, s_fin = rwkv_branch(p_rwkv, s0, P, emit)
    k, v = mla_kv(p_mla, P, ang)
    if not emit:
        return None, s_fin, (k, v)
    q = mla_q(p_mla, P, ang)
    if kv_prefix is not None:
        k_all = jnp.concatenate([kv_prefix[0], k], axis=1)
        v_all = jnp.concatenate([kv_prefix[1], v], axis=1)
    else:
        k_all, v_all = k, v
    o = blocked_attention(q, k_all, v_all).reshape(B, T, MLA_HEADS * V_HEAD)
    g_rwkv, g_mla = jnp.split(jax.nn.sigmoid(p_gate + P['gate_b']), N_BRANCH, axis=-1)
    merged = g_rwkv * (y_rwkv @ P['w_rwkv_proj']) + g_mla * (o @ P['w_mla_proj'])
    return merged @ P['w_out'], s_fin, (k, v)


def conv_ffn(h, P):
    u = depthwise_conv(h @ P['ffn_w_gate'], P['ffn_conv_w'], P['ffn_conv_b'])
    return (jax.nn.gelu(u, approximate=True) * (h @ P['ffn_w_val'])) @ P['ffn_w_down']


def setup_inputs(seed: int = 0) -> dict:
    key = jax.random.key(seed)
    ks = jax.random.split(key, 40)
    L, D, C, H = DEPTH, D_MODEL, RWKV_DIM, MLA_HEADS
    nrm = lambda k, shape, s: jax.random.normal(k, shape, jnp.float32) * s
    gain = lambda k, shape: 1.0 + 0.02 * jax.random.normal(k, shape, jnp.float32)
    return {
        'x': nrm(ks[0], (BATCH, SEQ, D), 1.0),
        'c': nrm(ks[1], (BATCH, D), 1.0),
        'ctx': nrm(ks[2], (BATCH, CTX_LEN, D), 1.0),
        'c_ctx': nrm(ks[3], (D,), 1.0),
        'w_ada': nrm(ks[4], (L, D, 6 * D), 0.5 * D ** -0.5),
        'b_ada': nrm(ks[5], (L, 6 * D), 0.02),
        'norm_mix_pre': gain(ks[6], (L, D)),
        'norm_mix_post': gain(ks[7], (L, D)),
        'norm_ffn_pre': gain(ks[8], (L, D)),
        'norm_ffn_post': gain(ks[9], (L, D)),
        'w_in': nrm(ks[10], (L, D, D_IN), D ** -0.5),
        'rwkv_mu': jax.random.uniform(ks[11], (L, RWKV_IN), jnp.float32),
        'rwkv_w0': nrm(ks[12], (L, N_DIR, C), 0.5),
        'rwkv_w2': nrm(ks[13], (L, N_DIR, DECAY_LORA, C), 0.5 * DECAY_LORA ** -0.5),
        'rwkv_a0': nrm(ks[14], (L, N_DIR, C), 0.5),
        'rwkv_a2': nrm(ks[15], (L, N_DIR, AAA_LORA, C), 0.5 * AAA_LORA ** -0.5),
        'rwkv_k_k': 0.85 + 0.05 * jax.random.normal(ks[16], (L, C), jnp.float32),
        'rwkv_k_a': gain(ks[17], (L, C)),
        'rwkv_r_k': nrm(ks[18], (L, RWKV_HEADS, RWKV_HEAD), 0.1),
        'rwkv_lnx_w': gain(ks[19], (L, C)),
        'rwkv_lnx_b': nrm(ks[20], (L, C), 0.02),
        'rwkv_g2': nrm(ks[21], (L, GATE_LORA, C), GATE_LORA ** -0.5),
        'w_rwkv_proj': nrm(ks[22], (L, C, D), C ** -0.5),
        'mla_q_norm': gain(ks[23], (L, Q_LORA)),
        'mla_kv_norm': gain(ks[24], (L, KV_LORA)),
        'mla_w_uq': nrm(ks[25], (L, Q_LORA, H * (QK_NOPE + QK_ROPE)), Q_LORA ** -0.5),
        'mla_w_ukv': nrm(ks[26], (L, KV_LORA, H * (QK_NOPE + V_HEAD)), KV_LORA ** -0.5),
        'w_mla_proj': nrm(ks[27], (L, H * V_HEAD, D), (H * V_HEAD) ** -0.5),
        'gate_b': nrm(ks[28], (L, GATE_IN), 0.02),
        'w_out': nrm(ks[29], (L, D, D), D ** -0.5),
        'ffn_w_gate': nrm(ks[30], (L, D, D_FF), D ** -0.5),
        'ffn_w_val': nrm(ks[31], (L, D, D_FF), D ** -0.5),
        'ffn_conv_w': nrm(ks[32], (L, CONV_W, D_FF), CONV_W ** -0.5),
        'ffn_conv_b': nrm(ks[33], (L, D_FF), 0.02),
        'ffn_w_down': nrm(ks[34], (L, D_FF, D), D_FF ** -0.5),
    }


def reference(x, c, ctx, c_ctx, w_ada, b_ada, norm_mix_pre, norm_mix_post, norm_ffn_pre, norm_ffn_post,
              w_in, rwkv_mu, rwkv_w0, rwkv_w2, rwkv_a0, rwkv_a2, rwkv_k_k, rwkv_k_a, rwkv_r_k,
              rwkv_lnx_w, rwkv_lnx_b, rwkv_g2, w_rwkv_proj, mla_q_norm, mla_kv_norm, mla_w_uq, mla_w_ukv,
              w_mla_proj, gate_b, w_out, ffn_w_gate, ffn_w_val, ffn_conv_w, ffn_conv_b, ffn_w_down):
    B = x.shape[0]
    ang = axial_angles(x.shape[1])
    xc = ctx
    for l in range(DEPTH):
        last = l == DEPTH - 1
        P = {
            'w_in': w_in[l], 'rwkv_mu': rwkv_mu[l], 'rwkv_w0': rwkv_w0[l], 'rwkv_w2': rwkv_w2[l],
            'rwkv_a0': rwkv_a0[l], 'rwkv_a2': rwkv_a2[l], 'rwkv_k_k': rwkv_k_k[l], 'rwkv_k_a': rwkv_k_a[l],
            'rwkv_r_k': rwkv_r_k[l], 'rwkv_lnx_w': rwkv_lnx_w[l], 'rwkv_lnx_b': rwkv_lnx_b[l],
            'rwkv_g2': rwkv_g2[l], 'w_rwkv_proj': w_rwkv_proj[l], 'mla_q_norm': mla_q_norm[l],
            'mla_kv_norm': mla_kv_norm[l], 'mla_w_uq': mla_w_uq[l], 'mla_w_ukv': mla_w_ukv[l],
            'w_mla_proj': w_mla_proj[l], 'gate_b': gate_b[l], 'w_out': w_out[l],
            'ffn_w_gate': ffn_w_gate[l], 'ffn_w_val': ffn_w_val[l], 'ffn_conv_w': ffn_conv_w[l],
            'ffn_conv_b': ffn_conv_b[l], 'ffn_w_down': ffn_w_down[l],
        }
        sh1, sc1, g1, sh2, sc2, g2 = [m[:, None, :] for m in jnp.split(jax.nn.silu(c) @ w_ada[l] + b_ada[l], 6, axis=-1)]
        ch1, cs1, cg1, ch2, cs2, cg2 = jnp.split(jax.nn.silu(c_ctx) @ w_ada[l] + b_ada[l], 6, axis=-1)
        h_ctx = modulate(rms_norm(xc, norm_mix_pre[l]), ch1, cs1)
        s0 = jnp.zeros((N_DIR, B, RWKV_HEADS, RWKV_HEAD, RWKV_HEAD), jnp.float32)
        out_ctx, s_ctx, kv_ctx = token_mixer(h_ctx, s0, None, None, P, emit=not last)
        h_lat = modulate(rms_norm(x, norm_mix_pre[l]), sh1, sc1)
        out_lat, _, _ = token_mixer(h_lat, s_ctx, ang, kv_ctx, P, emit=True)
        x = x + g1 * rms_norm(out_lat, norm_mix_post[l])
        h = modulate(rms_norm(x, norm_ffn_pre[l]), sh2, sc2)
        x = x + g2 * rms_norm(conv_ffn(h, P), norm_ffn_post[l])
        if not last:
            xc = xc + cg1 * rms_norm(out_ctx, norm_mix_post[l])
            hc = modulate(rms_norm(xc, norm_ffn_pre[l]), ch2, cs2)
            xc = xc + cg2 * rms_norm(conv_ffn(hc, P), norm_ffn_post[l])
    return x
```

```python
from contextlib import ExitStack
import os


class _Stop(Exception):
    pass


def _stop(level):
    if os.environ.get("PH3_STOP") == str(level):
        raise _Stop()

from concourse.bass_utils import run_bass_kernel_spmd
import numpy as np
import concourse.bass as bass
import concourse.mybir as mybir

F32 = mybir.dt.float32
BF16 = mybir.dt.bfloat16
I32 = mybir.dt.int32
AF = mybir.ActivationFunctionType
ALU = mybir.AluOpType
AX = mybir.AxisListType

COMPUTE = ("pe", "act", "dve", "pool")
NSEM = 4
NDSEM = 12


class Buf:
    __slots__ = ("name", "w", "r", "multi")

    def __init__(self, name, multi=False):
        self.name = name
        self.multi = multi
        self.w = []
        self.r = []


class Prog:
    def __init__(self, nc):
        self.nc = nc
        self.ops = {e: [] for e in ("pe", "act", "dve", "pool", "sp")}
        self.cnt = {e: 0 for e in COMPUTE}
        self.dcnt = {"sp": 0, "pool": 0, "act": 0}
        self.seen = {e: {} for e in self.ops}
        self.sems = {}
        self.nops = 0
        self.specials = []
        self.special_toks = []

    def _need(self, eng, tok, waits):
        kind, e2, i2 = tok
        key = (kind, e2)
        if kind == "c":
            if self.seen[eng].get(key, 0) >= i2:
                return
            self.seen[eng][key] = i2
            waits.append(tok)
        else:
            s = self.seen[eng].setdefault(key, set())
            if i2 in s:
                return
            s.add(i2)
            waits.append(tok)

    def op(self, eng, fn, reads=(), writes=(), pe_chain=False):
        waits = []
        for b in reads:
            for t in b.w:
                self._need(eng, t, waits)
        for b in writes:
            for t in b.w:
                if pe_chain and t[0] == "c" and t[1] == "pe" and eng == "pe":
                    continue
                self._need(eng, t, waits)
            for t in b.r:
                self._need(eng, t, waits)
        self.cnt[eng] += 1
        idx = self.cnt[eng]
        tok = ("c", eng, idx)
        self.seen[eng][("c", eng)] = max(self.seen[eng].get(("c", eng), 0), 0)
        for b in reads:
            b.r.append(tok)
        for b in writes:
            if pe_chain and b.w and all(t[0] == "c" and t[1] == "pe" for t in b.w):
                b.w = [tok]
            else:
                b.w = [tok]
            b.r = []
        self.ops[eng].append((waits, fn, tok))
        self.nops += 1
        return tok

    def dma(self, fn, reads=(), writes=(), q="sp"):
        eng = q
        waits = []
        for b in reads:
            for t in b.w:
                self._need(eng, t, waits)
        for b in writes:
            if b.multi:
                continue
            for t in b.w:
                self._need(eng, t, waits)
            for t in b.r:
                self._need(eng, t, waits)
        k = self.dcnt[q]
        self.dcnt[q] += 1
        if k >= NDSEM:
            self._need(eng, ("d", q, k - NDSEM), waits)
        tok = ("d", q, k)
        for b in reads:
            b.r.append(tok)
        for b in writes:
            if b.multi:
                b.w.append(tok)
            else:
                b.w = [tok]
                b.r = []
        self.ops[eng].append((waits, fn, tok))
        self.nops += 1
        return tok

    def special(self, eng, name, fn, reads=(), writes=()):
        waits = []
        for b in reads:
            for t in b.w:
                self._need(eng, t, waits)
        tok = ("s", name, 0)
        self.specials.append(name)
        for b in writes:
            b.w = [tok]
            b.r = []
        self.ops[eng].append((waits, fn, tok))
        self.special_toks.append((eng, tok))
        return tok

    def final_wait(self, eng, bufs):
        waits = []
        for b in bufs:
            for t in b.w:
                self._need(eng, t, waits)
        self.ops[eng].append((waits, None, None))

    def _sem_for(self, tok):
        kind, e, i = tok
        if kind == "c":
            return self.sems[("c", e, (i - 1) % NSEM)], (i - 1) // NSEM + 1, 1
        if kind == "s":
            return self.sems[("s", e)], 1, 1
        return self.sems[("d", e, i % NDSEM)], 16 * (i // NDSEM + 1), 16

    def emit(self, stack):
        nc = self.nc
        for eng, tok in self.special_toks:
            self.ops[eng].append(([tok], None, None))
        for e in COMPUTE:
            for j in range(NSEM):
                self.sems[("c", e, j)] = stack.enter_context(nc.semaphore(f"s_{e}_{j}"))
        for q in self.dcnt:
            if self.dcnt[q] == 0:
                continue
            for j in range(NDSEM):
                self.sems[("d", q, j)] = stack.enter_context(nc.semaphore(f"d_{q}_{j}"))
        for nm in self.specials:
            self.sems[("s", nm)] = stack.enter_context(nc.semaphore(f"x_{nm}"))
        stack.enter_context(nc.allow_non_contiguous_dma(reason="tiny pad-column stores and strided weight tiles"))
        block = stack.enter_context(nc.Block())

        def run(eng_name):
            def body(engine):
                for waits, fn, tok in self.ops[eng_name]:
                    for w in waits:
                        sem, val, _ = self._sem_for(w)
                        engine.wait_ge(sem, val)
                    if fn is None:
                        continue
                    ins = fn(engine)
                    sem, val, inc = self._sem_for(tok)
                    ins.then_inc(sem, inc)
            return body

        block.tensor(run("pe"))
        block.scalar(run("act"))
        block.vector(run("dve"))
        block.gpsimd(run("pool"))
        block.sync(run("sp"))


class Arena:
    def __init__(self, nc, stack, words):
        self.t = stack.enter_context(nc.sbuf_tensor("arena", [128, words], F32))
        self.words = words
        self.top = 0
        self.peak = 0
        self.recs = []

    def mark(self):
        return self.top

    def release(self, m):
        self.top = m

    def alloc2(self, name, free_shape, dtype=F32):
        n = int(np.prod(free_shape))
        esz = 2 if dtype == BF16 else 4
        w = (n * esz + 3) // 4
        w = (w + 7) // 8 * 8
        off = self.top
        self.top += w
        self.peak = max(self.peak, self.top)
        assert self.top <= self.words, f"arena overflow at {name}: {self.top} > {self.words}"
        buf = Buf(name)
        keep = []
        inh = []
        for (s0, e0, b0) in self.recs:
            if s0 < off + w and off < e0:
                inh.extend(b0.w)
                inh.extend(b0.r)
                if not (s0 >= off and e0 <= off + w):
                    keep.append((s0, e0, b0))
            else:
                keep.append((s0, e0, b0))
        seen = set()
        for t in inh:
            if t not in seen:
                seen.add(t)
                buf.w.append(t)
        keep.append((off, off + w, buf))
        self.recs = keep
        ap = self.t[:, off:off + w]
        if dtype != F32:
            ap = ap.bitcast(dtype)
        ap = ap[:, 0:n]
        if len(free_shape) == 2:
            ap = ap.rearrange("p (a b) -> p a b", a=free_shape[0])
        elif len(free_shape) == 3:
            ap = ap.rearrange("p (a b c) -> p a b c", a=free_shape[0], b=free_shape[1])
        return ap, buf
D = 2048; KC = 16; T = 4096; TCX = 256; TALL = T + TCX; NOWN = 1024; NHALO = NOWN + 2
DFF = 5632
EPS = 1e-6
DECAY_SCALE = float(np.exp(-0.5))
LNX_EPS = 64e-5
ATTN_SCALE = float(192 ** -0.5)
NCOL_A = 2240
C_R, C_K, C_V, C_WL, C_AL, C_CQ, C_CKV, C_KPE = 0, 256, 512, 768, 960, 1152, 1664, 2176


class VecPack:
    def __init__(self):
        self.items = []
        self.off = {}
        self.n = 0

    def add(self, name, arr):
        arr = np.ascontiguousarray(arr, dtype=np.float32).reshape(128, -1)
        self.off[name] = (self.n, arr.shape[1])
        self.items.append(arr)
        self.n += arr.shape[1]

    def build(self):
        return np.concatenate(self.items, axis=1)


def fm(v):
    v = np.asarray(v, np.float32)
    return v.reshape(-1, 128).T

class KB:
    def __init__(self, nc, stack):
        self.nc = nc
        self.P = Prog(nc)
        self.st = stack
        self.A = Arena(nc, stack, 47104)
        self.banks = []
        for i in range(8):
            t = stack.enter_context(nc.psum_tensor(f"psb{i}", [128, 512], F32))
            self.banks.append((t, Buf(f"psb{i}")))
        self.bi = 0
        self.held = set()
        self.cast_i = 0

    def ps(self, hold=False):
        while (self.bi % 8) in self.held:
            self.bi += 1
        i = self.bi % 8
        t, b = self.banks[i]
        self.bi += 1
        if hold:
            self.held.add(i)
        return t, b

    def unhold(self, b):
        for i, (t_, b_) in enumerate(self.banks):
            if b_ is b:
                self.held.discard(i)

    def _e(self, eng):
        return "dve" if eng == "pool" else eng

    def mm(self, out, lhsT, rhs, start, stop, r, w):
        self.P.op("pe", lambda e: e.matmul(out, lhsT=lhsT, rhs=rhs, start=start, stop=stop), reads=r, writes=[w], pe_chain=True)

    def act(self, out, in_, func, r, w, bias=None, scale=1.0):
        if bias is None:
            self.P.op("act", lambda e: e.activation(out=out, in_=in_, func=func, scale=scale), reads=r, writes=w)
        else:
            self.P.op("act", lambda e: e.activation(out=out, in_=in_, func=func, bias=bias, scale=scale), reads=r, writes=w)

    def tt(self, out, a, b, op, r, w, eng="dve"):
        eng = self._e(eng)
        self.P.op(eng, lambda e: e.tensor_tensor(out=out, in0=a, in1=b, op=op), reads=r, writes=w)

    def ts(self, out, a, s1, s2, op0, op1, r, w, eng="dve"):
        eng = self._e(eng)
        if s2 is None:
            self.P.op(eng, lambda e: e.tensor_scalar(out=out, in0=a, scalar1=s1, scalar2=None, op0=op0), reads=r, writes=w)
        else:
            self.P.op(eng, lambda e: e.tensor_scalar(out=out, in0=a, scalar1=s1, scalar2=s2, op0=op0, op1=op1), reads=r, writes=w)

    def stt(self, out, a, s, b, op0, op1, r, w):
        self.P.op("dve", lambda e: e.scalar_tensor_tensor(out=out, in0=a, scalar=s, in1=b, op0=op0, op1=op1), reads=r, writes=w)

    def copy(self, out, in_, r, w, eng="dve"):
        eng = self._e(eng)
        if eng == "act":
            self.P.op("act", lambda e: e.activation(out=out, in_=in_, func=AF.Copy), reads=r, writes=w)
        else:
            self.P.op(eng, lambda e: e.tensor_copy(out=out, in_=in_), reads=r, writes=w)

    def memset(self, out, val, w, eng="pool"):
        eng = self._e(eng)
        self.P.op(eng, lambda e: e.memset(out, val), reads=[], writes=w)

    def recip(self, out, in_, r, w):
        self.P.op("dve", lambda e: e.reciprocal(out=out, in_=in_), reads=r, writes=w)

    def load(self, out, src, w, r=(), q="sp"):
        self.P.dma(lambda e: e.dma_start(out=out, in_=src), reads=list(r), writes=w, q=q)

    def store(self, dst, src, r, w, q="sp"):
        q = "sp"
        self.P.dma(lambda e: e.dma_start(out=dst, in_=src), reads=r, writes=w, q=q)

    def rsqrt(self, out, in_, mul, add, r, wbuf):
        self.ts(out, in_, mul, add, ALU.mult, ALU.add, r, [wbuf])
        self.act(out, out, AF.Sqrt, [wbuf], [wbuf])
        self.recip(out, out, [wbuf], [wbuf])


class WStream:
    def __init__(self, kb, nst=3, kcmax=16, ncmax=128):
        self.kb = kb
        self.nst = nst
        self.f = [kb.A.alloc2(f"wsf{i}", [kcmax, ncmax]) for i in range(nst)]
        self.b = [kb.A.alloc2(f"wsb{i}", [kcmax, ncmax], BF16) for i in range(nst)]
        self.i = 0

    def get(self, W, kc, blk, cast_eng=None):
        kb = self.kb
        ncols = 128
        i = self.i % self.nst
        self.i += 1
        fa, fb = self.f[i]
        ba, bb = self.b[i]
        src = W[blk].rearrange("p (c n) -> p c n", c=kc)
        kb.load(fa[:, 0:kc, 0:ncols], src, [fb])
        eng = cast_eng or ("act" if (self.i % 2) else "dve")
        kb.copy(ba[:, 0:kc, 0:ncols], fa[:, 0:kc, 0:ncols], [fb], [bb], eng=eng)
        return ba, bb

    def stream(self, reqs):
        pend = [self.get(*reqs[0])]
        for i in range(len(reqs)):
            if i + 1 < len(reqs):
                pend.append(self.get(*reqs[i + 1]))
            yield pend.pop(0)

def tok_blocks(n0, n, bs):
    out = []
    t = n0
    while t < n0 + n:
        m = min(bs, n0 + n - t)
        out.append((t, m))
        t += m
    return out


def build_program(nc, voff, nvec, dbg=None):
    st = ExitStack()
    kb = KB(nc, st)
    P, A = kb.P, kb.A

    shapes = {
        "xT_all": [D, TALL], "xT_own": [D, NHALO], "vec": [128, nvec], "w_ada": [D, 6 * D], "w_inA": [20, 128, D],
        "w_gate_in": [32, 128, D], "w2": [2, 96, 256], "a2": [2, 96, 256], "g2": [256, 256], "w_uq": [512, 1024],
        "w_ukvk": [512, 512], "w_ukvv": [512, 512], "w_rp": [16, 128, 1024], "w_mp": [16, 128, D], "w_out": [16, 128, D],
        "w_fg": [44, 128, D], "w_fv": [44, 128, D], "w_fd": [64, 128, 1408], "ident": [128, 128], "masks": [128, 7 * 512],
        "ropeC": [64, TALL], "ropeS": [64, TALL],
    }

    class LazyDram(dict):
        def __missing__(self, name):
            ap = nc.dram_tensor(name, list(shapes[name]), F32, kind="ExternalInput").ap()
            self[name] = ap
            return ap
    dram = LazyDram()
    xT_all = dram["xT_all"]; vec_d = dram["vec"]; w_ada = dram["w_ada"]; w_inA = dram["w_inA"]; ident_d = dram["ident"]
    oT = nc.dram_tensor("oT", [D, NOWN], F32, kind="ExternalOutput").ap()
    P1 = nc.dram_tensor("P1", [2560, TALL + 2], F32, kind="Internal").ap()
    XBs = [nc.dram_tensor(f"XBs{i}", [128, T], BF16, kind="Internal").ap() for i in range(6)]
    XBd = [nc.dram_tensor(f"XBd{i}", [512, T], BF16, kind="Internal").ap() for i in range(6)]
    X1 = nc.dram_tensor("X1", [D, NHALO], F32, kind="Internal").ap()
    bP1 = Buf("P1", True); bXBs = [Buf(f"XBs{i}", True) for i in range(6)]; bXBd = [Buf(f"XBd{i}") for i in range(6)]; bX1 = Buf("X1", True); bOT = Buf("oT", True)
    dbg_out = None
    if dbg is not None:
        dbg_out = nc.dram_tensor("dbg", list(dbg), F32, kind="ExternalOutput").ap()

    vec, bvec = A.alloc2("vec", [nvec])
    kb.load(vec, vec_d, [bvec])

    def V(name, i=0, n=1):
        o, w = voff[name]
        return vec[:, o + i:o + i + n]

    ones_bf, bones = A.alloc2("ones_bf", [128], BF16)
    kb.memset(ones_bf, 1.0, [bones])
    ident, bident = A.alloc2("ident", [128])
    kb.load(ident, ident_d, [bident])
    der, bder = A.alloc2("der", [16 * 10])

    def DV(k, c):
        return der[:, k * 16 + c:k * 16 + c + 1]
    A1, B1, A1C, B1C, A2, B2, G1, G2 = range(8)
    mark0 = A.mark()

    sT, bsT = A.alloc2("sT", [32])
    modv, bmod = A.alloc2("modv", [192])
    o, _ = voff["cT"]
    kb.act(sT, vec[:, o:o + 32], AF.Sigmoid, [bvec], [bsT])
    kb.tt(sT, sT, vec[:, o:o + 32], ALU.mult, [bsT, bvec], [bsT])
    m = A.mark()
    wa = [A.alloc2(f"wada{i}", [16, 512]) for i in range(2)]
    pm, bpm = kb.ps(hold=True)
    for nb in range(24):
        wt, wb = wa[nb % 2]
        kb.load(wt, w_ada[:, nb * 512:(nb + 1) * 512].rearrange("(c p) n -> p c n", p=128), [wb])
        for j in range(4):
            nn = nb * 4 + j
            for kc in range(16):
                kb.mm(pm[:, nn * 2:nn * 2 + 2], wt[:, kc, j * 128:(j + 1) * 128], sT[:, kc * 2:kc * 2 + 2], kc == 0, kc == 15, [wb, bsT], bpm)
    ob, _ = voff["b_ada"]
    mv3 = modv.rearrange("p (n r) -> p n r", r=2)
    pm3 = pm[:, 0:192].rearrange("p (n r) -> p n r", r=2)
    bb3 = vec[:, ob:ob + 96].rearrange("p (n r) -> p n r", r=1).to_broadcast([128, 96, 2])
    kb.tt(mv3, pm3, bb3, ALU.add, [bpm, bvec], [bmod])
    kb.unhold(bpm)
    A.release(m)

    def MOD(j6, row):
        return mv3[:, j6 * 16:(j6 + 1) * 16, row]
    d3 = der.rearrange("p (k c) -> p k c", c=16)
    onp, _ = voff["npre"]; onq, _ = voff["npost"]; onf, _ = voff["nfpre"]; ong, _ = voff["nfpost"]
    kb.stt(d3[:, A1, :], MOD(1, 0), 1.0, vec[:, onp:onp + 16], ALU.add, ALU.mult, [bmod, bvec], [bder])
    kb.copy(d3[:, B1, :], MOD(0, 0), [bmod], [bder])
    kb.stt(d3[:, A1C, :], MOD(1, 1), 1.0, vec[:, onp:onp + 16], ALU.add, ALU.mult, [bmod, bvec], [bder])
    kb.copy(d3[:, B1C, :], MOD(0, 1), [bmod], [bder])
    kb.stt(d3[:, A2, :], MOD(4, 0), 1.0, vec[:, onf:onf + 16], ALU.add, ALU.mult, [bmod, bvec], [bder])
    kb.copy(d3[:, B2, :], MOD(3, 0), [bmod], [bder])
    kb.tt(d3[:, G1, :], MOD(2, 0), vec[:, onq:onq + 16], ALU.mult, [bmod, bvec], [bder])
    kb.tt(d3[:, G2, :], MOD(5, 0), vec[:, ong:ong + 16], ALU.mult, [bmod, bvec], [bder])

    cxe = dict(kb=kb, P=P, A=A, st=st, P1=P1, bP1=bP1, der=der, bder=bder)
    if os.environ.get("PH0_STOP") == "0":
        return cxe
    def norm_mod(xb, bxb, n, hout, bh, sqb, bsq, rs, brs, Ak, Bk):
        kb.act(sqb[:, :, 0:n], xb[:, :, 0:n], AF.Square, [bxb], [bsq])
        pp, bpp = kb.ps()
        for c in range(16):
            kb.mm(pp[:, 0:n], ones_bf, sqb[:, c, 0:n], c == 0, c == 15, [bones, bsq], bpp)
        kb.rsqrt(rs[:, 0:n], pp[:, 0:n], 1.0 / D, EPS, [bpp], brs)
        for c in range(16):
            kb.stt(xb[:, c, 0:n], xb[:, c, 0:n], DV(Ak, c), rs[:, 0:n], ALU.mult, ALU.mult, [bxb, bder, brs], [bxb])
            kb.act(hout[:, c, :], xb[:, c, 0:n], AF.Identity, [bxb, bder], [bh], bias=DV(Bk, c))

    hT, bhT = A.alloc2("hT_all", [16, TALL], BF16)
    m = A.mark()
    xbs = [A.alloc2(f"xb{i}", [16, 256]) for i in range(2)]
    sqb, bsq = A.alloc2("sqb", [16, 256], BF16)
    rs, brs = A.alloc2("rs", [256])
    xall3 = xT_all.rearrange("(c p) t -> p c t", p=128)
    for bi, (t0, n) in enumerate(tok_blocks(0, TALL, 256)):
        xb, bxb = xbs[bi % 2]
        kb.load(xb[:, :, 0:n], xall3[:, :, t0:t0 + n], [bxb])
        isctx = t0 < TCX
        norm_mod(xb, bxb, n, hT[:, :, t0:t0 + n], bhT, sqb, bsq, rs, brs, A1C if isctx else A1, B1C if isctx else B1)
    A.release(m)

    if os.environ.get("PH0_STOP") == "1":
        return cxe
    m = A.mark()
    ws = WStream(kb, nst=3)
    stg = [A.alloc2(f"stg{i}", [512]) for i in range(3)]
    si = 0
    blocks = [(0, 256)] + tok_blocks(256, T, 512)
    zp, bzp = A.alloc2("zp", [2])
    kb.memset(zp, 0.0, [bzp])
    _v = os.environ.get("P2VAR", "")
    for ci in range(20):
        if "nopad" in _v:
            break
        kb.store(P1[ci * 128:(ci + 1) * 128, 0:1], zp[:, 0:1], [bzp], [bP1])
        kb.store(P1[ci * 128:(ci + 1) * 128, TALL + 1:TALL + 2], zp[:, 1:2], [bzp], [bP1])
    for ci, (wbf, bw) in enumerate(ws.stream([(w_inA, 16, ci) for ci in range(20)])):
        for (t0, n) in blocks:
            pp, bpp = kb.ps()
            for c in range(16):
                kb.mm(pp[:, 0:n], wbf[:, c, :], hT[:, c, t0:t0 + n], c == 0, c == 15, [bw, bhT], bpp)
            sg, bsg = stg[si % 3]
            kb.copy(sg[:, 0:n], pp[:, 0:n], [bpp], [bsg], eng=("act" if si % 2 else "dve"))
            si += 1
            kb.store(P1[ci * 128:(ci + 1) * 128, 1 + t0:1 + t0 + n], sg[:, 0:n], [bsg], [bP1], q=("sp" if "spstore" in _v else "pool"))
    A.release(m)
    A.release(mark0)
    ctx = dict(kb=kb, P=P, A=A, V=V, vec=vec, bvec=bvec, DV=DV, der=der, bder=bder, ident=ident, bident=bident,
               ones_bf=ones_bf, bones=bones, P1=P1, bP1=bP1, XBs=XBs, bXBs=bXBs, XBd=XBd, bXBd=bXBd, X1=X1, bX1=bX1,
               oT=oT, bOT=bOT, dram=dram, norm_mod=norm_mod, voff=voff, st=st, dbg_out=dbg_out,
               consts=(A1, B1, A1C, B1C, A2, B2, G1, G2))
    return ctx

def phase3_rwkv(cx):
    kb, P, A, V, vec, bvec = cx["kb"], cx["P"], cx["A"], cx["V"], cx["vec"], cx["bvec"]
    P1, bP1, XBs, bXBs = cx["P1"], cx["bP1"], cx["XBs"], cx["bXBs"]
    ident, bident = cx["ident"], cx["bident"]
    dram, voff = cx["dram"], cx["voff"]
    m_phase = A.mark()

    def T_(name, shape, dt=F32):
        return A.alloc2(name, shape, dt)

    msk, bmsk = T_("masks", [7 * 512])
    kb.load(msk, dram["masks"], [bmsk])
    SU, SL, IU, IL, I8 = [msk[:, i * 512:(i + 1) * 512] for i in range(5)]
    onesblk = msk[:, 5 * 512:5 * 512 + 128]
    om, bom = T_("om", [12]); hm, bhm = T_("hm", [12]); omka, bomka = T_("omka", [2])
    omu, _ = voff["mus"]
    kb.ts(om, vec[:, omu:omu + 12], -1.0, 1.0, ALU.mult, ALU.add, [bvec], [bom])
    kb.ts(hm, vec[:, omu:omu + 12], 0.5, None, ALU.mult, None, [bvec], [bhm])
    oka, _ = voff["k_a"]
    kb.ts(omka, vec[:, oka:oka + 2], -1.0, 1.0, ALU.mult, ALU.add, [bvec], [bomka])
    w2f, bw2f = T_("w2f", [2, 256]); a2f, ba2f = T_("a2f", [2, 256]); g2f, bg2f = T_("g2f", [2, 256])
    w2b, bw2b = T_("w2b", [2, 256], BF16); a2b, ba2b = T_("a2b", [2, 256], BF16); g2b, bg2b = T_("g2b", [2, 256], BF16)
    kb.load(w2f[0:96], dram["w2"].rearrange("d r c -> r d c"), [bw2f])
    kb.load(a2f[0:96], dram["a2"].rearrange("d r c -> r d c"), [ba2f])
    kb.load(g2f, dram["g2"].rearrange("(c p) n -> p c n", p=128), [bg2f])
    kb.copy(w2b[0:96], w2f[0:96], [bw2f], [bw2b]); kb.copy(a2b[0:96], a2f[0:96], [ba2f], [ba2b]); kb.copy(g2b, g2f, [bg2f], [bg2b])

    def shift_lerp(Z, bZ, n, out, bout, mi, col, tmp, btmp, parts=128):
        p = slice(0, parts)
        kb.tt(tmp[p, 0:n], Z[p, 0:n], Z[p, 2:n + 2], ALU.add, [bZ], [btmp], eng="pool")
        kb.ts(tmp[p, 0:n], tmp[p, 0:n], hm[p, mi * 2 + col:mi * 2 + col + 1], None, ALU.mult, None, [btmp, bhm], [btmp])
        kb.stt(out, Z[p, 1:n + 1], om[p, mi * 2 + col:mi * 2 + col + 1], tmp[p, 0:n], ALU.mult, ALU.add, [bZ, bom, btmp], [bout])

    def load_halo(Z, bZ, row0, nrows, t0, n, left_edge, right_edge):
        kb.load(Z[0:nrows, 0:n + 2], P1[row0:row0 + nrows, t0:t0 + n + 2], [bZ], r=[bP1])
        if left_edge:
            kb.memset(Z[0:nrows, 0:1], 0.0, [bZ])
        if right_edge:
            kb.memset(Z[0:nrows, n + 1:n + 2], 0.0, [bZ])

    windows = [(0, 256)] + [(256 + 512 * i, 512) for i in range(8)]
    NW = 512
    for hp in range(2):
        m_hp = A.mark()
        Yacc, bY = T_("Yacc", [T]); Bacc, bB = T_("Bacc", [T]); Gt, bG = T_("Gt", [T])
        for d in range(2):
            m_d = A.mark()
            MS, MST, MI = (SU, SL, IU) if d == 0 else (SL, SU, IL)
            Hst, bH = T_("Hst", [64])
            kb.memset(Hst, 0.0, [bH])
            order = windows if d == 0 else [windows[0]] + windows[:0:-1]
            for (t0, n) in order:
                m_w = A.mark()
                isctx = t0 < TCX
                nch = n // 64
                le = t0 in (0, TCX); re = (t0 + n) in (TCX, TALL)
                Zr, bZr = T_("Zr", [NW + 2]); Zk, bZk = T_("Zk", [NW + 2]); Zv, bZv = T_("Zv", [NW + 2])
                Zw, bZw = T_("Zw", [NW + 2]); Za, bZa = T_("Za", [NW + 2])
                tmp, btmp = T_("tmp", [NW]); tmp2, btmp2 = T_("tmp2", [NW])
                r_s, br = T_("r_s", [NW]); k_s, bk = T_("k_s", [NW]); VT, bVT = T_("VT", [64 + NW])
                load_halo(Zr, bZr, 0 + hp * 128, 128, t0, n, le, re)
                load_halo(Zk, bZk, 256 + hp * 128, 128, t0, n, le, re)
                load_halo(Zv, bZv, 512 + hp * 128, 128, t0, n, le, re)
                load_halo(Zw, bZw, 768 + d * 96, 96, t0, n, le, re)
                load_halo(Za, bZa, 960 + d * 96, 96, t0, n, le, re)
                shift_lerp(Zr, bZr, n, r_s[:, 0:n], br, 0, hp, tmp, btmp)
                shift_lerp(Zk, bZk, n, k_s[:, 0:n], bk, 1, hp, tmp, btmp)
                kb.memset(VT[:, 0:64], 0.0, [bVT])
                shift_lerp(Zv, bZv, n, VT[:, 64:64 + n], bVT, 2, hp, tmp, btmp)
                v_s = VT[:, 64:64 + NW]
                wl_s, bwl = T_("wl_s", [NW]); twl, btwl = T_("twl", [NW], BF16); alb, balb = T_("alb", [NW], BF16)
                shift_lerp(Zw, bZw, n, wl_s[0:96, 0:n], bwl, 3, d, tmp, btmp, parts=96)
                kb.act(twl[0:96, 0:n], wl_s[0:96, 0:n], AF.Tanh, [bwl], [btwl])
                shift_lerp(Za, bZa, n, wl_s[0:96, 0:n], bwl, 4, d, tmp, btmp, parts=96)
                kb.copy(alb[0:96, 0:n], wl_s[0:96, 0:n], [bwl], [balb], eng="pool")
                _stop(1)
                sg, bsg = T_("sg", [NW]); asig, bas = T_("asig", [NW])
                pw, bpw = kb.ps(); pa, bpa = kb.ps()
                kb.mm(pw[:, 0:n], w2b[0:96, d, hp * 128:(hp + 1) * 128], twl[0:96, 0:n], True, True, [bw2b, btwl], bpw)
                kb.mm(pa[:, 0:n], a2b[0:96, d, hp * 128:(hp + 1) * 128], alb[0:96, 0:n], True, True, [ba2b, balb], bpa)
                kb.act(sg[:, 0:n], pw[:, 0:n], AF.Sigmoid, [bpw, bvec], [bsg], bias=V("w0", d * 2 + hp))
                kb.act(asig[:, 0:n], pa[:, 0:n], AF.Sigmoid, [bpa, bvec], [bas], bias=V("a0", d * 2 + hp))
                kk, bkk = T_("kk", [NW]); kd, bkd = T_("kd", [NW]); bb_, bbb = T_("b", [NW])
                kb.ts(kk[:, 0:n], k_s[:, 0:n], V("k_k", hp), None, ALU.mult, None, [bk, bvec], [bkk])
                kb.tt(tmp[:, 0:n], kk[:, 0:n], kk[:, 0:n], ALU.mult, [bkk], [btmp], eng="pool")
                pk, bpk = kb.ps()
                kb.mm(pk[:, 0:n], onesblk, tmp[:, 0:n], True, True, [bmsk, btmp], bpk)
                kb.ts(tmp2[:, 0:n], pk[:, 0:n], 1e-24, None, ALU.max, None, [bpk], [btmp2])
                kb.act(tmp2[:, 0:n], tmp2[:, 0:n], AF.Sqrt, [btmp2], [btmp2])
                kb.recip(tmp2[:, 0:n], tmp2[:, 0:n], [btmp2], [btmp2])
                kb.tt(kk[:, 0:n], kk[:, 0:n], tmp2[:, 0:n], ALU.mult, [bkk, btmp2], [bkk])
                kb.ts(tmp[:, 0:n], asig[:, 0:n], V("k_a", hp), omka[:, hp:hp + 1], ALU.mult, ALU.add, [bas, bvec, bomka], [btmp])
                kb.tt(kd[:, 0:n], k_s[:, 0:n], tmp[:, 0:n], ALU.mult, [bk, btmp], [bkd])
                kb.tt(bb_[:, 0:n], kk[:, 0:n], asig[:, 0:n], ALU.mult, [bkk, bas], [bbb], eng="pool")
                if not isctx:
                    tl = t0 - TCX
                    kb.stt(tmp[:, 0:n], r_s[:, 0:n], V("r_k", hp), kd[:, 0:n], ALU.mult, ALU.mult, [br, bvec, bkd], [btmp])
                    pb, bpb = kb.ps()
                    kb.mm(pb[:, 0:n], onesblk, tmp[:, 0:n], True, True, [bmsk, btmp], bpb)
                    if d == 0:
                        kb.stt(Bacc[:, tl:tl + n], pb[:, 0:n], 0.5, v_s[:, 0:n], ALU.mult, ALU.mult, [bpb, bVT], [bB])
                    else:
                        kb.stt(tmp2[:, 0:n], pb[:, 0:n], 0.5, v_s[:, 0:n], ALU.mult, ALU.mult, [bpb, bVT], [btmp2])
                        kb.tt(Bacc[:, tl:tl + n], Bacc[:, tl:tl + n], tmp2[:, 0:n], ALU.add, [bB, btmp2], [bB], eng="pool")
                    if d == 0:
                        Zg, bZg = T_("Zg", [2, NW + 2]); glb, bglb = T_("glb", [2, NW], BF16)
                        for c in range(2):
                            kb.load(Zg[:, c, 0:n + 2], P1[1152 + c * 128:1152 + (c + 1) * 128, t0:t0 + n + 2], [bZg], r=[bP1])
                        if le:
                            kb.memset(Zg[:, :, 0:1], 0.0, [bZg])
                        if re:
                            kb.memset(Zg[:, :, n + 1:n + 2], 0.0, [bZg])
                        for c in range(2):
                            shift_lerp(Zg[:, c, :], bZg, n, tmp2[:, 0:n], btmp2, 5, c, tmp, btmp)
                            kb.act(glb[:, c, 0:n], tmp2[:, 0:n], AF.Sigmoid, [btmp2], [bglb])
                        pg, bpg = kb.ps()
                        for c in range(2):
                            kb.mm(pg[:, 0:n], g2b[:, c, hp * 128:(hp + 1) * 128], glb[:, c, 0:n], c == 0, c == 1, [bg2b, bglb], bpg)
                        kb.copy(Gt[:, tl:tl + n], pg[:, 0:n], [bpg], [bG], eng="act")
                _stop(2)
                csA, bcA = T_("csA", [NW]); csB, bcB = T_("csB", [NW])
                src, bsrc = sg, bsg
                dsts = [(csA, bcA), (csB, bcB)]
                for si_, s in enumerate((1, 2, 4, 8, 16, 32)):
                    dst, bdst = dsts[si_ % 2]
                    s3 = src[:, 0:n].rearrange("p (c t) -> p c t", t=64)
                    d3 = dst[:, 0:n].rearrange("p (c t) -> p c t", t=64)
                    if d == 0:
                        kb.tt(d3[:, :, s:], s3[:, :, s:], s3[:, :, :64 - s], ALU.add, [bsrc], [bdst])
                        kb.copy(d3[:, :, :s], s3[:, :, :s], [bsrc], [bdst], eng="pool")
                    else:
                        kb.tt(d3[:, :, :64 - s], s3[:, :, :64 - s], s3[:, :, s:], ALU.add, [bsrc], [bdst])
                        kb.copy(d3[:, :, 64 - s:], s3[:, :, 64 - s:], [bsrc], [bdst], eng="pool")
                    src, bsrc = dst, bdst
                cs, bcs = src, bsrc
                cs3 = cs[:, 0:n].rearrange("p (c t) -> p c t", t=64)
                endc = 63 if d == 0 else 0
                E1, bE1 = T_("E1", [NW]); E2, bE2 = T_("E2", [NW]); E3, bE3 = T_("E3", [NW]); E4, bE4 = T_("E4", [NW])
                kb.act(E1[:, 0:n], cs[:, 0:n], AF.Exp, [bcs], [bE1], scale=-DECAY_SCALE)
                kb.act(E2[:, 0:n], cs[:, 0:n], AF.Exp, [bcs], [bE2], scale=DECAY_SCALE)
                kb.tt(tmp[:, 0:n], cs[:, 0:n], sg[:, 0:n], ALU.subtract, [bcs, bsg], [btmp], eng="pool")
                kb.act(E3[:, 0:n], tmp[:, 0:n], AF.Exp, [btmp], [bE3], scale=-DECAY_SCALE)
                t3 = tmp2[:, 0:n].rearrange("p (c t) -> p c t", t=64)
                kb.tt(t3, cs3, cs3[:, :, endc:endc + 1].to_broadcast([128, nch, 64]), ALU.subtract, [bcs], [btmp2])
                kb.act(E4[:, 0:n], tmp2[:, 0:n], AF.Exp, [btmp2], [bE4], scale=DECAY_SCALE)
                pC = E1[:, 0:n].rearrange("p (c t) -> p c t", t=64)
                RT, bRT = T_("RT", [NW]); AT, bAT = T_("AT", [NW]); BKT, bBKT = T_("BKT", [8, 128]); BKpT, bBKpT = T_("BKpT", [8, 128])
                kb.tt(RT[:, 0:n], r_s[:, 0:n], E1[:, 0:n], ALU.mult, [br, bE1], [bRT], eng="pool")
                kb.stt(AT[:, 0:n], kk[:, 0:n], -1.0, E3[:, 0:n], ALU.mult, ALU.mult, [bkk, bE3], [bAT])
                b3 = bb_[:, 0:n].rearrange("p (c t) -> p c t", t=64); kd3 = kd[:, 0:n].rearrange("p (c t) -> p c t", t=64)
                e23 = E2[:, 0:n].rearrange("p (c t) -> p c t", t=64); e43 = E4[:, 0:n].rearrange("p (c t) -> p c t", t=64)
                kb.tt(BKT[:, 0:nch, 0:64], b3, e23, ALU.mult, [bbb, bE2], [bBKT])
                kb.tt(BKT[:, 0:nch, 64:128], kd3, e23, ALU.mult, [bkd, bE2], [bBKT], eng="pool")
                kb.tt(BKpT[:, 0:nch, 0:64], b3, e43, ALU.mult, [bbb, bE4], [bBKpT])
                kb.tt(BKpT[:, 0:nch, 64:128], kd3, e43, ALU.mult, [bkd, bE4], [bBKpT], eng="pool")

                def chunk(tile_, j):
                    return tile_[:, j * 64:(j + 1) * 64]
                _stop(3)
                W = nch * 64

                def PT(name):
                    return T_(name, [8, 64])
                LAK, bLAK = PT("LAK"); LRu, bLRu = PT("LRu"); LRv, bLRv = PT("LRv")
                Bp, bBp = PT("Bp"); Kp, bKp = PT("Kp"); Vtm, bVtm = PT("Vtm"); Utm, bUtm = PT("Utm")
                Xa, bXa = PT("Xa"); XTa, bXTa = PT("XTa"); Xb, bXb = PT("Xb"); XTb, bXTb = PT("XTb")
                Qa, bQa = PT("Qa"); Qb, bQb = PT("Qb")

                def fl(t_):
                    return t_.rearrange("p c t -> p (c t)")[:, 0:W]

                def packed(lhs_fn, rhs_fn, rbufs):
                    pp, bpp = kb.ps()
                    for hh in range(2):
                        h_ = slice(hh * 64, hh * 64 + 64)
                        for j in range(nch):
                            kb.mm(pp[h_, j * 64:(j + 1) * 64], lhs_fn(h_, j), rhs_fn(h_, j), True, True, rbufs, bpp)
                    return pp, bpp
                AT_c = lambda h_, j: chunk(AT, j)[h_]
                RT_c = lambda h_, j: chunk(RT, j)[h_]
                idh = lambda h_, j: ident[h_, h_]
                pN, bpN = packed(lambda h_, j: BKT[h_, j, 0:64], AT_c, [bBKT, bAT])
                kb.tt(fl(Xa), pN[:, 0:W], MS[:, 0:W], ALU.mult, [bpN, bmsk], [bXa])
                pNT, bpNT = packed(AT_c, lambda h_, j: BKT[h_, j, 0:64], [bBKT, bAT])
                kb.tt(fl(XTa), pNT[:, 0:W], MST[:, 0:W], ALU.mult, [bpNT, bmsk], [bXTa])
                pp, bpp = packed(lambda h_, j: BKT[h_, j, 64:128], AT_c, [bBKT, bAT])
                kb.tt(fl(LAK), pp[:, 0:W], MS[:, 0:W], ALU.mult, [bpp, bmsk], [bLAK])
                if not isctx:
                    pp, bpp = packed(lambda h_, j: BKT[h_, j, 0:64], RT_c, [bBKT, bRT])
                    kb.tt(fl(LRu), pp[:, 0:W], MI[:, 0:W], ALU.mult, [bpp, bmsk], [bLRu])
                    pp, bpp = packed(lambda h_, j: BKT[h_, j, 64:128], RT_c, [bBKT, bRT])
                    kb.tt(fl(LRv), pp[:, 0:W], MI[:, 0:W], ALU.mult, [bpp, bmsk], [bLRv])
                pp, bpp = packed(lambda h_, j: BKpT[h_, j, 0:64], idh, [bBKpT, bident])
                kb.copy(fl(Bp), pp[:, 0:W], [bpp], [bBp], eng="act")
                pp, bpp = packed(lambda h_, j: BKpT[h_, j, 64:128], idh, [bBKpT, bident])
                kb.copy(fl(Kp), pp[:, 0:W], [bpp], [bKp], eng="act")
                pp, bpp = packed(lambda h_, j: VT[h_, 64 + j * 64:64 + (j + 1) * 64], idh, [bVT, bident])
                kb.copy(fl(Vtm), pp[:, 0:W], [bpp], [bVtm], eng="act")
                kb.tt(fl(Qa), fl(Xa), I8[:, 0:W], ALU.add, [bXa, bmsk], [bQa], eng="pool")
                X, bX, XT, bXT, Q, bQ = Xa, bXa, XTa, bXTa, Qa, bQa
                Xn, bXn, XTn, bXTn, Qn, bQn = Xb, bXb, XTb, bXTb, Qb, bQb
                for lev in range(1, 6):
                    pB, bpB = packed(lambda h_, j: X[h_, j, :], lambda h_, j: XT[h_, j, :], [bX, bXT])
                    kb.copy(fl(XTn), pB[:, 0:W], [bpB], [bXTn], eng="act")
                    if lev < 5:
                        pA, bpA = packed(lambda h_, j: XT[h_, j, :], lambda h_, j: X[h_, j, :], [bX, bXT])
                        kb.copy(fl(Xn), pA[:, 0:W], [bpA], [bXn], eng="dve")
                    pQ, bpQ = packed(lambda h_, j: XTn[h_, j, :], lambda h_, j: Q[h_, j, :], [bXTn, bQ])
                    kb.tt(fl(Qn), pQ[:, 0:W], fl(Q), ALU.add, [bpQ, bQ], [bQn])
                    X, bX, Xn, bXn = Xn, bXn, X, bX
                    XT, bXT, XTn, bXTn = XTn, bXTn, XT, bXT
                    Q, bQ, Qn, bQn = Qn, bQn, Q, bQ
                TT, bTT = Q, bQ
                _stop(5)
                Wsb, bWsb = T_("Wsb", [64])
                pY, bpY = (None, None) if isctx else kb.ps(hold=True)
                jorder = range(nch) if d == 0 else range(nch - 1, -1, -1)
                for j in jorder:
                    pW, bpW = kb.ps()
                    for hh in range(2):
                        h_ = slice(hh * 64, hh * 64 + 64)
                        kb.mm(pW[h_, 0:64], chunk(AT, j)[h_], Hst[h_, :], True, False, [bAT, bH], bpW)
                        kb.mm(pW[h_, 0:64], LAK[h_, j, :], Vtm[h_, j, :], False, True, [bLAK, bVtm], bpW)
                    kb.copy(Wsb, pW[:, 0:64], [bpW], [bWsb], eng="act")
                    pU, bpU = kb.ps()
                    for hh in range(2):
                        h_ = slice(hh * 64, hh * 64 + 64)
                        kb.mm(pU[h_, 0:64], TT[h_, j, :], Wsb[h_, :], True, True, [bTT, bWsb], bpU)
                    kb.copy(Utm[:, j, :], pU[:, 0:64], [bpU], [bUtm], eng="dve")
                    if not isctx:
                        for hh in range(2):
                            h_ = slice(hh * 64, hh * 64 + 64)
                            o_ = pY[h_, j * 64:(j + 1) * 64]
                            kb.mm(o_, Hst[h_, :], chunk(RT, j)[h_], True, False, [bH, bRT], bpY)
                            kb.mm(o_, Utm[h_, j, :], LRu[h_, j, :], False, False, [bUtm, bLRu], bpY)
                            kb.mm(o_, Vtm[h_, j, :], LRv[h_, j, :], False, True, [bVtm, bLRv], bpY)
                    pH, bpH = kb.ps()
                    for hh in range(2):
                        h_ = slice(hh * 64, hh * 64 + 64)
                        kb.mm(pH[h_, 0:64], Bp[h_, j, :], Utm[h_, j, :], True, False, [bBp, bUtm], bpH)
                        kb.mm(pH[h_, 0:64], Kp[h_, j, :], Vtm[h_, j, :], False, True, [bKp, bVtm], bpH)
                    kb.stt(Hst, Hst, pC[:, j, endc:endc + 1], pH[:, 0:64], ALU.mult, ALU.add, [bH, bE1, bpH], [bH])
                if not isctx:
                    tl = t0 - TCX
                    if d == 0:
                        kb.copy(Yacc[:, tl:tl + n], pY[:, 0:n], [bpY], [bY], eng="act")
                    else:
                        kb.tt(Yacc[:, tl:tl + n], Yacc[:, tl:tl + n], pY[:, 0:n], ALU.add, [bY, bpY], [bY])
                    kb.unhold(bpY)
                _stop(6)
                A.release(m_w)
            _stop(7)
            A.release(m_d)
        m_f = A.mark()
        yc, byc = T_("yc", [512]); sq, bsq = T_("sq", [512]); rsd, brsd = T_("rsd", [512]); yo, byo = T_("yo", [512], BF16)
        for (t0, n) in tok_blocks(0, T, 512):
            pm_, bpm_ = kb.ps()
            kb.mm(pm_[:, 0:n], onesblk, Yacc[:, t0:t0 + n], True, True, [bmsk, bY], bpm_)
            kb.stt(yc, pm_[:, 0:n], -1.0 / 64, Yacc[:, t0:t0 + n], ALU.mult, ALU.add, [bpm_, bY], [byc])
            kb.tt(sq, yc, yc, ALU.mult, [byc], [bsq], eng="pool")
            pv_, bpv_ = kb.ps()
            kb.mm(pv_[:, 0:n], onesblk, sq, True, True, [bmsk, bsq], bpv_)
            kb.rsqrt(rsd, pv_[:, 0:n], 1.0 / 64, LNX_EPS, [bpv_], brsd)
            kb.tt(yc, yc, rsd, ALU.mult, [byc, brsd], [byc])
            kb.ts(yc, yc, V("lnx_w", hp), V("lnx_b", hp), ALU.mult, ALU.add, [byc, bvec], [byc])
            kb.tt(yc, yc, Bacc[:, t0:t0 + n], ALU.add, [byc, bB], [byc], eng="pool")
            kb.tt(yo, yc, Gt[:, t0:t0 + n], ALU.mult, [byc, bG], [byo])
            kb.store(XBs[hp][:, t0:t0 + n], yo, [byo], [bXBs[hp]])
        A.release(m_f)
        A.release(m_hp)
    A.release(m_phase)

def phase4_mla(cx):
    kb, P, A, V, vec, bvec = cx["kb"], cx["P"], cx["A"], cx["V"], cx["vec"], cx["bvec"]
    P1, bP1, XBs, bXBs = cx["P1"], cx["bP1"], cx["XBs"], cx["bXBs"]
    ones_bf, bones = cx["ones_bf"], cx["bones"]
    dram = cx["dram"]
    m_phase = A.mark()

    def T_(name, shape, dt=F32):
        return A.alloc2(name, shape, dt)
    C_CQ, C_CKV, C_KPE, C_KPESW = 1408, 1920, 2432, 2496
    Kn, bKn = T_("Kn", [4, TALL], BF16); Kr, bKr = T_("Kr", [TALL], BF16); Vt, bVt = T_("Vt", [34, 512], BF16)
    kb.memset(Kr[64:128, :], 0.0, [bKr])
    wq, bwq = T_("wq", [4, 1024], BF16); wkk, bwkk = T_("wkk", [4, 512], BF16); wkv, bwkv = T_("wkv", [4, 512], BF16)
    m = A.mark()
    wf, bwf = T_("wf", [4, 1024])
    kb.load(wf, dram["w_uq"].rearrange("(c p) n -> p c n", p=128), [bwf]); kb.copy(wq, wf, [bwf], [bwq], eng="pool")
    kb.load(wf[:, :, 0:512], dram["w_ukvk"].rearrange("(c p) n -> p c n", p=128), [bwf]); kb.copy(wkk, wf[:, :, 0:512], [bwf], [bwkk], eng="pool")
    kb.load(wf[:, :, 0:512], dram["w_ukvv"].rearrange("(c p) n -> p c n", p=128), [bwf]); kb.copy(wkv, wf[:, :, 0:512], [bwf], [bwkv], eng="pool")
    A.release(m)
    xin, bxin = T_("xin", [4, 512]); sqb, bsq = T_("sqb4", [4, 512], BF16); rs, brs = T_("rs4", [512]); cn, bcn = T_("cn", [4, 512], BF16)
    kp, bkp = T_("kp", [512]); kps, bkps = T_("kps", [512]); rc, brc = T_("rc", [512]); rsn, brsn = T_("rsn", [512])
    t1, bt1 = T_("t1", [512]); t2, bt2 = T_("t2", [512])

    def rms4(row0, t0, n, gname):
        kb.load(xin[:, :, 0:n], P1[row0:row0 + 512, 1 + t0:1 + t0 + n].rearrange("(c p) t -> p c t", p=128), [bxin], r=[bP1])
        kb.act(sqb[:, :, 0:n], xin[:, :, 0:n], AF.Square, [bxin], [bsq])
        pp, bpp = kb.ps()
        for c in range(4):
            kb.mm(pp[:, 0:n], ones_bf, sqb[:, c, 0:n], c == 0, c == 3, [bones, bsq], bpp)
        kb.rsqrt(rs[:, 0:n], pp[:, 0:n], 1.0 / 512, EPS, [bpp], brs)
        for c in range(4):
            kb.stt(cn[:, c, 0:n], xin[:, c, 0:n], V(gname, c), rs[:, 0:n], ALU.mult, ALU.mult, [bxin, bvec, brs], [bcn])

    def rope(row_x, row_sw, t0, n, out, bout):
        kb.load(kp[0:64, 0:n], P1[row_x:row_x + 64, 1 + t0:1 + t0 + n], [bkp], r=[bP1])
        kb.load(kps[0:64, 0:n], P1[row_sw:row_sw + 64, 1 + t0:1 + t0 + n], [bkps], r=[bP1])
        kb.load(rc[0:64, 0:n], dram["ropeC"][:, t0:t0 + n], [brc])
        kb.load(rsn[0:64, 0:n], dram["ropeS"][:, t0:t0 + n], [brsn])
        kb.tt(t1[0:64, 0:n], kp[0:64, 0:n], rc[0:64, 0:n], ALU.mult, [bkp, brc], [bt1])
        kb.tt(t2[0:64, 0:n], kps[0:64, 0:n], rsn[0:64, 0:n], ALU.mult, [bkps, brsn], [bt2], eng="pool")
        kb.tt(out, t1[0:64, 0:n], t2[0:64, 0:n], ALU.add, [bt1, bt2], [bout])

    ei = 0
    for (t0, n) in [(0, 256)] + tok_blocks(256, T, 512):
        rms4(C_CKV, t0, n, "kvn")
        for h in range(4):
            pp, bpp = kb.ps()
            for c in range(4):
                kb.mm(pp[:, 0:n], wkk[:, c, h * 128:(h + 1) * 128], cn[:, c, 0:n], c == 0, c == 3, [bwkk, bcn], bpp)
            kb.copy(Kn[:, h, t0:t0 + n], pp[:, 0:n], [bpp], [bKn], eng=("act" if ei % 2 else "dve")); ei += 1
        for tt_ in range(n // 128):
            pp, bpp = kb.ps()
            for c in range(4):
                kb.mm(pp[:, 0:512], cn[:, c, tt_ * 128:(tt_ + 1) * 128], wkv[:, c, :], c == 0, c == 3, [bwkv, bcn], bpp)
            kb.copy(Vt[:, (t0 // 128) + tt_, :], pp[:, 0:512], [bpp], [bVt], eng=("act" if ei % 2 else "dve")); ei += 1
        rope(C_KPE, C_KPESW, t0, n, Kr[0:64, t0:t0 + n], bKr)
    Qn, bQn = T_("Qn", [4, 512], BF16); Qr, bQr = T_("Qr", [4, 512], BF16)
    kb.memset(Qr[64:128, :, :], 0.0, [bQr])
    qa, bqa = T_("qa", [512]); qb, bqb = T_("qb", [512])
    pts = [T_(f"pt{i}", [512], BF16) for i in range(3)]
    rden, brden = T_("rden", [512]); ob, bob = T_("ob", [512], BF16)
    pti = 0
    for qi in range(8):
        t0 = TCX + qi * 512
        rms4(C_CQ, t0, 512, "qn")
        kb.load(rc[0:64, :], dram["ropeC"][:, t0:t0 + 512], [brc])
        kb.load(rsn[0:64, :], dram["ropeS"][:, t0:t0 + 512], [brsn])
        for h in range(4):
            pp, bpp = kb.ps()
            for c in range(4):
                kb.mm(pp[:, :], wq[:, c, h * 256:h * 256 + 128], cn[:, c, :], c == 0, c == 3, [bwq, bcn], bpp)
            kb.copy(Qn[:, h, :], pp[:, :], [bpp], [bQn], eng="act")
            p1_, bp1_ = kb.ps(); p2_, bp2_ = kb.ps()
            for c in range(4):
                kb.mm(p1_[0:64, :], wq[:, c, h * 256 + 128:h * 256 + 192], cn[:, c, :], c == 0, c == 3, [bwq, bcn], bp1_)
            for c in range(4):
                kb.mm(p2_[0:64, :], wq[:, c, h * 256 + 192:h * 256 + 256], cn[:, c, :], c == 0, c == 3, [bwq, bcn], bp2_)
            kb.tt(qa[0:64, :], p1_[0:64, :], rc[0:64, :], ALU.mult, [bp1_, brc], [bqa])
            kb.tt(qb[0:64, :], p2_[0:64, :], rsn[0:64, :], ALU.mult, [bp2_, brsn], [bqb])
            kb.tt(Qr[0:64, h, :], qa[0:64, :], qb[0:64, :], ALU.add, [bqa, bqb], [bQr], eng="pool")
        for h in range(4):
            po, bpo = kb.ps(hold=True); pd, bpd = kb.ps(hold=True)
            for kt in range(34):
                pS, bpS = kb.ps()
                kb.mm(pS[:, :], Kn[:, h, kt * 128:(kt + 1) * 128], Qn[:, h, :], True, False, [bKn, bQn], bpS)
                kb.mm(pS[:, :], Kr[:, kt * 128:(kt + 1) * 128], Qr[:, h, :], False, True, [bKr, bQr], bpS)
                pt, bpt = pts[pti % 3]; pti += 1
                kb.act(pt, pS[:, :], AF.Exp, [bpS], [bpt], scale=ATTN_SCALE)
                kb.mm(po[:, :], Vt[:, kt, h * 128:(h + 1) * 128], pt, kt == 0, kt == 33, [bVt, bpt], bpo)
                kb.mm(pd[:, :], ones_bf, pt, kt == 0, kt == 33, [bones, bpt], bpd)
            kb.recip(rden, pd[:, :], [bpd], [brden])
            kb.tt(ob, po[:, :], rden, ALU.mult, [bpo, brden], [bob])
            kb.unhold(bpo); kb.unhold(bpd)
            kb.store(XBs[2 + h][:, qi * 512:(qi + 1) * 512], ob, [bob], [bXBs[2 + h]])
    A.release(m_phase)

def phase_exchange(cx, chunks):
    P = cx["P"]
    XBs, bXBs, XBd, bXBd = cx["XBs"], cx["bXBs"], cx["XBd"], cx["bXBd"]
    for c in chunks:
        if os.environ.get("SKIP_CC") == "1":
            continue
        def fn(e, c=c):
            return e.collective_compute("AllGather", ALU.bypass, replica_groups=[[0, 1, 2, 3], [4, 5, 6, 7]], ins=[XBs[c]], outs=[XBd[c]])
        P.special("pool", f"cc{c}", fn, reads=[bXBs[c]], writes=[bXBd[c]])


def phase56(cx):
    kb, P, A, V, vec, bvec = cx["kb"], cx["P"], cx["A"], cx["V"], cx["vec"], cx["bvec"]
    XBd, bXBd, X1, bX1, oT, bOT = cx["XBd"], cx["bXBd"], cx["X1"], cx["bX1"], cx["oT"], cx["bOT"]
    ones_bf, bones = cx["ones_bf"], cx["bones"]
    DV, der, bder = cx["DV"], cx["der"], cx["bder"]
    norm_mod = cx["norm_mod"]
    A1, B1, A1C, B1C, A2, B2, G1, G2 = cx["consts"]
    dram, voff = cx["dram"], cx["voff"]
    TB = [(0, 342), (342, 342), (684, 342)]
    xown3 = dram["xT_own"].rearrange("(c p) t -> p c t", p=128)

    def T_(name, shape, dt=F32):
        return A.alloc2(name, shape, dt)
    m_phase = A.mark()
    ws = WStream(kb, nst=3)
    m5 = A.mark()
    hTo, bhTo = T_("hTo", [16, NHALO], BF16)
    Yf, bYf = T_("Yf", [8, NHALO], BF16); Of, bOf = T_("Of", [16, NHALO], BF16)
    m = A.mark()
    xblk, bxblk = T_("xblk", [16, 342]); sqb, bsq = T_("sqb5", [16, 342], BF16); rs, brs = T_("rs5", [342])
    for (t0, n) in TB:
        kb.load(xblk[:, :, 0:n], xown3[:, :, t0:t0 + n], [bxblk])
        norm_mod(xblk, bxblk, n, hTo[:, :, t0:t0 + n], bhTo, sqb, bsq, rs, brs, A1, B1)
    A.release(m)
    m = A.mark()
    lds = [T_(f"ld{i}", [NHALO], BF16) for i in range(3)]
    li = 0
    osel, _ = voff["sel"]
    for r in range(4):
        for fc in range(6):
            dst = Yf[:, r * 2 + fc, :] if fc < 2 else Of[:, r * 4 + (fc - 2), :]
            bdst = bYf if fc < 2 else bOf
            for j in range(4):
                ld, bld = lds[li % 3]; li += 1
                c0 = max(0, 1024 * j - 1); c1 = min(T, 1024 * j + NHALO - 1)
                o0 = c0 - (1024 * j - 1)
                if j == 0:
                    kb.memset(ld[:, 0:1], 0.0, [bld])
                if j == 3:
                    kb.memset(ld[:, NHALO - 1:NHALO], 0.0, [bld])
                kb.load(ld[:, o0:o0 + (c1 - c0)], XBd[fc][r * 128:(r + 1) * 128, c0:c1], [bld], r=[bXBd[fc]])
                if j == 0:
                    kb.ts(dst, ld, vec[:, osel:osel + 1], None, ALU.mult, None, [bld, bvec], [bdst])
                else:
                    kb.stt(dst, ld, vec[:, osel + j:osel + j + 1], dst, ALU.mult, ALU.add, [bld, bvec, bdst], [bdst])
    A.release(m)
    mrg, bmrg = T_("mrg", [16, NHALO], BF16)
    m = A.mark()
    M1, bM1 = T_("M1", [NHALO]); M2, bM2 = T_("M2", [NHALO]); GA, bGA = T_("GA", [NHALO])
    ogb, _ = voff["gate_b"]
    _reqs = []
    for jc in range(16):
        _reqs += [(dram["w_rp"], 8, jc), (dram["w_gate_in"], 16, jc),
                  (dram["w_mp"], 16, jc), (dram["w_gate_in"], 16, 16 + jc)]
    _wg = ws.stream(_reqs)
    for jc in range(16):
        def prod(W, kc, col0, rhs_t, brhs, outt, bout, sig_bias=None):
            wbf, bw = next(_wg)
            for (t0, n) in TB:
                pp, bpp = kb.ps()
                for c in range(kc):
                    kb.mm(pp[:, 0:n], wbf[:, c, :], rhs_t[:, c, t0:t0 + n], c == 0, c == kc - 1, [bw, brhs], bpp)
                if sig_bias is None:
                    kb.copy(outt[:, t0:t0 + n], pp[:, 0:n], [bpp], [bout], eng="act")
                else:
                    kb.act(outt[:, t0:t0 + n], pp[:, 0:n], AF.Sigmoid, [bpp, bvec], [bout], bias=sig_bias)
        prod(dram["w_rp"], 8, jc * 128, Yf, bYf, M1, bM1)
        prod(dram["w_gate_in"], 16, jc * 128, hTo, bhTo, GA, bGA, sig_bias=vec[:, ogb + jc:ogb + jc + 1])
        kb.tt(M1, M1, GA, ALU.mult, [bM1, bGA], [bM1])
        prod(dram["w_mp"], 16, jc * 128, Of, bOf, M2, bM2)
        prod(dram["w_gate_in"], 16, 2048 + jc * 128, hTo, bhTo, GA, bGA, sig_bias=vec[:, ogb + 16 + jc:ogb + 16 + jc + 1])
        kb.tt(M2, M2, GA, ALU.mult, [bM2, bGA], [bM2], eng="pool")
        kb.tt(mrg[:, jc, :], M1, M2, ALU.add, [bM1, bM2], [bmrg])
    A.release(m)
    A.release(m5)
    mrg2, bmrg2 = T_("mrg", [16, NHALO], BF16)
    kb.copy(mrg2, mrg, [bmrg], [bmrg2], eng="pool")
    m5b = A.mark()
    outT, boutT = T_("outT", [16, NHALO])
    for jc, (wbf, bw) in enumerate(ws.stream([(dram["w_out"], 16, jc) for jc in range(16)])):
        for (t0, n) in TB:
            pp, bpp = kb.ps()
            for c in range(16):
                kb.mm(pp[:, 0:n], wbf[:, c, :], mrg2[:, c, t0:t0 + n], c == 0, c == 15, [bw, bmrg2], bpp)
            kb.copy(outT[:, jc, t0:t0 + n], pp[:, 0:n], [bpp], [boutT], eng=("act" if jc % 2 else "dve"))
    m = A.mark()
    xblk, bxblk = T_("xblk", [16, 342]); sqb, bsq = T_("sqb5", [16, 342], BF16); rs, brs = T_("rs5", [342])
    hT2, bhT2 = mrg2, bmrg2
    for (t0, n) in TB:
        kb.act(sqb[:, :, 0:n], outT[:, :, t0:t0 + n], AF.Square, [boutT], [bsq])
        pp, bpp = kb.ps()
        for c in range(16):
            kb.mm(pp[:, 0:n], ones_bf, sqb[:, c, 0:n], c == 0, c == 15, [bones, bsq], bpp)
        kb.rsqrt(rs[:, 0:n], pp[:, 0:n], 1.0 / D, EPS, [bpp], brs)
        kb.load(xblk[:, :, 0:n], xown3[:, :, t0:t0 + n], [bxblk])
        for c in range(16):
            kb.stt(outT[:, c, t0:t0 + n], outT[:, c, t0:t0 + n], DV(G1, c), rs[:, 0:n], ALU.mult, ALU.mult, [boutT, bder, brs], [boutT])
        kb.tt(outT[:, :, t0:t0 + n], outT[:, :, t0:t0 + n], xblk[:, :, 0:n], ALU.add, [boutT, bxblk], [boutT], eng="pool")
        kb.store(X1.rearrange("(c p) t -> p c t", p=128)[:, :, t0:t0 + n], outT[:, :, t0:t0 + n], [boutT], [bX1])
        kb.copy(xblk[:, :, 0:n], outT[:, :, t0:t0 + n], [boutT], [bxblk], eng="pool")
        norm_mod(xblk, bxblk, n, hT2[:, :, t0:t0 + n], bhT2, sqb, bsq, rs, brs, A2, B2)
    A.release(m5b)
    h2, bh2 = hT2, bhT2
    yacc, byacc = T_("yacc", [16, NOWN]); m6 = A.mark(); agrp, bagrp = T_("agrp", [11, NOWN], BF16)
    upre, bup = T_("upre", [NHALO]); u, bu = T_("u", [NOWN]); ge, bge = T_("ge", [NOWN])
    ocw, _ = voff["conv_w"]; ocb, _ = voff["conv_b"]; ohm, _ = voff["hmask"]
    OB = [(0, 512), (512, 512)]
    _reqs = []
    for grp in range(4):
        for f in range(11):
            _reqs += [(dram["w_fg"], 16, grp * 11 + f), (dram["w_fv"], 16, grp * 11 + f)]
        _reqs += [(dram["w_fd"], 11, grp * 16 + nc_) for nc_ in range(16)]
    _wg = ws.stream(_reqs)
    for grp in range(4):
        for f in range(11):
            ff = grp * 11 + f
            wbf, bw = next(_wg)
            for (t0, n) in TB:
                pp, bpp = kb.ps()
                for c in range(16):
                    kb.mm(pp[:, 0:n], wbf[:, c, :], h2[:, c, t0:t0 + n], c == 0, c == 15, [bw, bh2], bpp)
                kb.copy(upre[:, t0:t0 + n], pp[:, 0:n], [bpp], [bup], eng="act")
            kb.ts(upre[:, 0:1], upre[:, 0:1], vec[:, ohm:ohm + 1], None, ALU.mult, None, [bup, bvec], [bup])
            kb.ts(upre[:, NHALO - 1:NHALO], upre[:, NHALO - 1:NHALO], vec[:, ohm + 1:ohm + 2], None, ALU.mult, None, [bup, bvec], [bup])
            kb.ts(u, upre[:, 0:NOWN], vec[:, ocw + ff:ocw + ff + 1], vec[:, ocb + ff:ocb + ff + 1], ALU.mult, ALU.add, [bup, bvec], [bu])
            kb.stt(u, upre[:, 1:NOWN + 1], vec[:, ocw + 44 + ff:ocw + 44 + ff + 1], u, ALU.mult, ALU.add, [bup, bvec, bu], [bu])
            kb.stt(u, upre[:, 2:NOWN + 2], vec[:, ocw + 88 + ff:ocw + 88 + ff + 1], u, ALU.mult, ALU.add, [bup, bvec, bu], [bu])
            kb.act(ge, u, AF.Gelu_apprx_tanh, [bu], [bge])
            wbf, bw = next(_wg)
            for (t0, n) in OB:
                pp, bpp = kb.ps()
                for c in range(16):
                    kb.mm(pp[:, 0:n], wbf[:, c, :], h2[:, c, 1 + t0:1 + t0 + n], c == 0, c == 15, [bw, bh2], bpp)
                kb.tt(agrp[:, f, t0:t0 + n], pp[:, 0:n], ge[:, t0:t0 + n], ALU.mult, [bpp, bge], [bagrp])
        for nc_ in range(16):
            wbf, bw = next(_wg)
            for (t0, n) in OB:
                pp, bpp = kb.ps()
                for f in range(11):
                    kb.mm(pp[:, 0:n], wbf[:, f, :], agrp[:, f, t0:t0 + n], f == 0, f == 10, [bw, bagrp], bpp)
                if grp == 0:
                    kb.copy(yacc[:, nc_, t0:t0 + n], pp[:, 0:n], [bpp], [byacc], eng="act")
                else:
                    kb.tt(yacc[:, nc_, t0:t0 + n], yacc[:, nc_, t0:t0 + n], pp[:, 0:n], ALU.add, [byacc, bpp], [byacc])
    A.release(m6)
    yacc2, byacc2 = yacc, byacc
    x1b, bx1b = T_("x1b", [16, 256]); sq6, bsq6 = T_("sq6", [16, 256], BF16); rs6, brs6 = T_("rs6", [256])
    X13 = X1.rearrange("(c p) t -> p c t", p=128)
    oT3 = oT.rearrange("(c p) t -> p c t", p=128)
    for (t0, n) in [(0, 256), (256, 256), (512, 256), (768, 256)]:
        kb.act(sq6, yacc2[:, :, t0:t0 + n], AF.Square, [byacc2], [bsq6])
        pp, bpp = kb.ps()
        for c in range(16):
            kb.mm(pp[:, 0:n], ones_bf, sq6[:, c, :], c == 0, c == 15, [bones, bsq6], bpp)
        kb.rsqrt(rs6, pp[:, 0:n], 1.0 / D, EPS, [bpp], brs6)
        kb.load(x1b, X13[:, :, 1 + t0:1 + t0 + n], [bx1b], r=[bX1])
        for c in range(16):
            kb.stt(yacc2[:, c, t0:t0 + n], yacc2[:, c, t0:t0 + n], DV(G2, c), rs6, ALU.mult, ALU.mult, [byacc2, bder, brs6], [byacc2])
        kb.tt(x1b, x1b, yacc2[:, :, t0:t0 + n], ALU.add, [bx1b, byacc2], [bx1b], eng="pool")
        kb.store(oT3[:, :, t0:t0 + n], x1b, [bx1b], [bOT])
    A.release(m_phase)

_ROPE_PARTNER = np.array([(j + 16) if (j % 32) < 16 else (j - 16) for j in range(64)])


def _rope_tables():
    half = 32
    freqs = (np.float32(10000.0) ** (-np.arange(0, half, 2, dtype=np.float32) / np.float32(half))).astype(np.float32)
    tpos = np.arange(T)
    row = (tpos // 64).astype(np.float32); col = (tpos % 64).astype(np.float32)
    ar = row[:, None] * freqs; ac = col[:, None] * freqs
    C = np.ones((64, TALL), np.float32); S = np.zeros((64, TALL), np.float32)
    for j in range(64):
        ang = ar[:, j % 16] if j < 32 else ac[:, j % 16]
        C[j, TCX:] = np.cos(ang)
        S[j, TCX:] = (-np.sin(ang)) if (j % 32) < 16 else np.sin(ang)
    return C, S


def _masks():
    r = (np.arange(128) % 64)[:, None]; c = (np.arange(512) % 64)[None, :]
    M = np.zeros((128, 7 * 512), np.float32)
    M[:, 0:512] = (r < c); M[:, 512:1024] = (r > c); M[:, 1024:1536] = (r <= c); M[:, 1536:2048] = (r >= c); M[:, 2048:2560] = (r == c)
    M[:, 2560:2688] = ((np.arange(128) // 64)[:, None] == (np.arange(128) // 64)[None, :])
    return M


def _blk(W, kc):
    K, M = W.shape
    assert K == kc * 128 and M % 128 == 0
    return np.ascontiguousarray(W.reshape(kc, 128, M // 128, 128).transpose(2, 1, 0, 3).reshape(M // 128, 128, kc * 128))


def _prep_core(inp, b, g, shared):
    f32 = np.float32
    vp = VecPack()
    cT = np.stack([fm(inp["c"][b]), fm(inp["c_ctx"])], axis=2)
    vp.add("cT", cT.reshape(128, 32))
    vp.add("b_ada", fm(inp["b_ada"][0]))
    vp.add("npre", fm(inp["norm_mix_pre"][0])); vp.add("npost", fm(inp["norm_mix_post"][0]))
    vp.add("nfpre", fm(inp["norm_ffn_pre"][0])); vp.add("nfpost", fm(inp["norm_ffn_post"][0]))
    vp.add("gate_b", fm(inp["gate_b"][0]))
    vp.add("conv_w", np.concatenate([fm(inp["ffn_conv_w"][0][j]) for j in range(3)], axis=1))
    vp.add("conv_b", fm(inp["ffn_conv_b"][0]))
    vp.add("hmask", np.tile(np.array([[1.0 if g > 0 else 0.0, 1.0 if g < 3 else 0.0]], f32), (128, 1)))
    sel = np.zeros((128, 4), f32); sel[:, g] = 1.0
    vp.add("sel", sel)
    mu = inp["rwkv_mu"][0]
    ch0 = 256 * g

    def pad96(v):
        o = np.zeros(128, f32); o[:96] = v
        return o
    mus = []
    for base in (0, 1024, 2048):
        for hp in range(2):
            mus.append(mu[base + ch0 + hp * 128: base + ch0 + (hp + 1) * 128])
    for base in (3072, 3264):
        for d in range(2):
            mus.append(pad96(mu[base + d * 96: base + (d + 1) * 96]))
    for c in range(2):
        mus.append(mu[3456 + c * 128:3456 + (c + 1) * 128])
    vp.add("mus", np.stack(mus, axis=1))

    def own2(v):
        return np.stack([v[ch0 + hp * 128: ch0 + (hp + 1) * 128] for hp in range(2)], axis=1)
    vp.add("w0", np.stack([inp["rwkv_w0"][0][d][ch0 + hp * 128: ch0 + (hp + 1) * 128] for d in range(2) for hp in range(2)], axis=1))
    vp.add("a0", np.stack([inp["rwkv_a0"][0][d][ch0 + hp * 128: ch0 + (hp + 1) * 128] for d in range(2) for hp in range(2)], axis=1))
    vp.add("k_k", own2(inp["rwkv_k_k"][0])); vp.add("k_a", own2(inp["rwkv_k_a"][0]))
    vp.add("r_k", own2(inp["rwkv_r_k"][0].reshape(-1)))
    vp.add("lnx_w", own2(inp["rwkv_lnx_w"][0])); vp.add("lnx_b", own2(inp["rwkv_lnx_b"][0]))
    vp.add("qn", fm(inp["mla_q_norm"][0])); vp.add("kvn", fm(inp["mla_kv_norm"][0]))
    w_in = inp["w_in"][0]
    cols = np.concatenate([np.arange(ch0, ch0 + 256), 1024 + np.arange(ch0, ch0 + 256), 2048 + np.arange(ch0, ch0 + 256),
                           np.arange(3072, 3712), np.arange(3712, 4800), 4736 + _ROPE_PARTNER])
    assert cols.shape[0] == 2560
    heads = range(4 * g, 4 * g + 4)
    uq_cols = np.concatenate([np.concatenate([h * 192 + np.arange(192), h * 192 + 128 + _ROPE_PARTNER]) for h in heads])
    ukvk = np.concatenate([h * 256 + np.arange(128) for h in heads]); ukvv = np.concatenate([h * 256 + 128 + np.arange(128) for h in heads])
    x_b = inp["x"][b]
    idx = np.clip(1024 * g - 1 + np.arange(NHALO), 0, T - 1)
    m = dict(shared)
    m.update({
        "xT_all": shared["_xT_all"][b], "xT_own": np.ascontiguousarray(x_b[idx].T),
        "vec": vp.build(),
        "w_inA": _blk(np.ascontiguousarray(w_in[:, cols]), 16),
        "w2": np.ascontiguousarray(inp["rwkv_w2"][0][:, :, ch0:ch0 + 256]), "a2": np.ascontiguousarray(inp["rwkv_a2"][0][:, :, ch0:ch0 + 256]),
        "g2": np.ascontiguousarray(inp["rwkv_g2"][0][:, ch0:ch0 + 256]),
        "w_uq": np.ascontiguousarray(inp["mla_w_uq"][0][:, uq_cols]),
        "w_ukvk": np.ascontiguousarray(inp["mla_w_ukv"][0][:, ukvk]), "w_ukvv": np.ascontiguousarray(inp["mla_w_ukv"][0][:, ukvv]),
    })
    for k in [k for k in m if k.startswith("_")]:
        del m[k]
    return m, vp


_CACHE = {}


def _get_program(voff, nvec, upto=99, dump=None):
    key = ("nc", upto, dump)
    if key not in _CACHE:
        nc = bass.Bass("TRN2", target_bir_lowering=False)
        cx = build_program(nc, voff, nvec)
        if upto >= 3:
            try:
                phase3_rwkv(cx)
            except _Stop:
                pass
        if upto >= 5:
            phase_exchange(cx, [0, 1])
        if upto >= 4:
            phase4_mla(cx)
        if upto >= 5:
            phase_exchange(cx, [2, 3, 4, 5])
        if upto >= 6:
            phase56(cx)
        if dump is None:
            cx["P"].final_wait("sp", [cx["bOT"]])
        else:
            dn = dump.split(":")
            if dn[0] == "der":
                dbg = nc.dram_tensor("dbg", [128, 160], F32, kind="ExternalOutput").ap()
                bdbg = Buf("dbg")
                cx["kb"].store(dbg, cx["der"], [cx["bder"]], [bdbg])
                cx["P"].final_wait("sp", [bdbg])
                cx["P"].emit(cx["st"]); cx["st"].close()
                _CACHE[key] = nc
                _CACHE["stats"] = (cx["P"].nops, cx["A"].peak, {k: len(v) for k, v in cx["P"].ops.items()})
                return nc
            src_ap, src_buf = cx[dn[0]], cx["b" + dn[0]]
            if len(dn) == 3:
                src_ap = src_ap[int(dn[1]):int(dn[2])]
            dt = BF16 if dn[0] in ("XBs", "XBd") else F32
            dbg = nc.dram_tensor("dbg", list(src_ap.shape), dt, kind="ExternalOutput").ap()
            bdbg = Buf("dbg")
            cx["kb"].store(dbg, src_ap, [src_buf], [bdbg])
            cx["P"].final_wait("sp", [bdbg])
        cx["P"].emit(cx["st"])
        cx["st"].close()
        _CACHE[key] = nc
        _CACHE["stats"] = (cx["P"].nops, cx["A"].peak, {k: len(v) for k, v in cx["P"].ops.items()})
    return _CACHE[key]


def kernel(**inputs):
    inp = {k: np.asarray(v, dtype=np.float32) for k, v in inputs.items()}
    C, S = _rope_tables()
    shared = {
        "w_ada": np.ascontiguousarray(inp["w_ada"][0]), "w_gate_in": _blk(np.ascontiguousarray(inp["w_in"][0][:, 4800:8896]), 16),
        "w_rp": _blk(inp["w_rwkv_proj"][0], 8), "w_mp": _blk(inp["w_mla_proj"][0], 16),
        "w_out": _blk(inp["w_out"][0], 16), "w_fg": _blk(inp["ffn_w_gate"][0], 16),
        "w_fv": _blk(inp["ffn_w_val"][0], 16),
        "w_fd": np.concatenate([_blk(inp["ffn_w_down"][0][gq * 1408:(gq + 1) * 1408], 11) for gq in range(4)], axis=0),
        "ident": np.eye(128, dtype=np.float32), "masks": _masks(), "ropeC": C, "ropeS": S,
        "_xT_all": [np.ascontiguousarray(np.concatenate([inp["ctx"][b], inp["x"][b]], axis=0).T) for b in range(2)],
    }
    in_maps = []
    vp = None
    for core in range(8):
        m, vp = _prep_core(inp, core // 4, core % 4, shared)
        in_maps.append(m)
    nc = _get_program(vp.off, vp.n)
    res = run_bass_kernel_spmd(nc, in_maps, core_ids=list(range(8)))
    out = np.empty((2, T, D), np.float32)
    for core in range(8):
        b, g = core // 4, core % 4
        out[b, 1024 * g:1024 * (g + 1), :] = res.results[core]["oT"].T
    return out
```

```python
from contextlib import ExitStack
import os


class _Stop(Exception):
    pass


def _stop(level):
    if os.environ.get("PH3_STOP") == str(level):
        raise _Stop()

from concourse.bass_utils import run_bass_kernel_spmd
import numpy as np
import concourse.bass as bass
import concourse.mybir as mybir

F32 = mybir.dt.float32
BF16 = mybir.dt.bfloat16
I32 = mybir.dt.int32
AF = mybir.ActivationFunctionType
ALU = mybir.AluOpType
AX = mybir.AxisListType

COMPUTE = ("pe", "act", "dve", "pool")
NSEM = 4
NDSEM = 12


class Buf:
    __slots__ = ("name", "w", "r", "multi")

    def __init__(self, name, multi=False):
        self.name = name
        self.multi = multi
        self.w = []
        self.r = []


class Prog:
    def __init__(self, nc):
        self.nc = nc
        self.ops = {e: [] for e in ("pe", "act", "dve", "pool", "sp")}
        self.cnt = {e: 0 for e in COMPUTE}
        self.dcnt = {"sp": 0, "pool": 0, "act": 0}
        self.seen = {e: {} for e in self.ops}
        self.sems = {}
        self.nops = 0
        self.specials = []
        self.special_toks = []

    def _need(self, eng, tok, waits):
        kind, e2, i2 = tok
        key = (kind, e2)
        if kind == "c":
            if self.seen[eng].get(key, 0) >= i2:
                return
            self.seen[eng][key] = i2
            waits.append(tok)
        else:
            s = self.seen[eng].setdefault(key, set())
            if i2 in s:
                return
            s.add(i2)
            waits.append(tok)

    def op(self, eng, fn, reads=(), writes=(), pe_chain=False):
        waits = []
        for b in reads:
            for t in b.w:
                self._need(eng, t, waits)
        for b in writes:
            for t in b.w:
                if pe_chain and t[0] == "c" and t[1] == "pe" and eng == "pe":
                    continue
                self._need(eng, t, waits)
            for t in b.r:
                self._need(eng, t, waits)
        self.cnt[eng] += 1
        idx = self.cnt[eng]
        tok = ("c", eng, idx)
        self.seen[eng][("c", eng)] = max(self.seen[eng].get(("c", eng), 0), 0)
        for b in reads:
            b.r.append(tok)
        for b in writes:
            if pe_chain and b.w and all(t[0] == "c" and t[1] == "pe" for t in b.w):
                b.w = [tok]
            else:
                b.w = [tok]
            b.r = []
        self.ops[eng].append((waits, fn, tok))
        self.nops += 1
        return tok

    def dma(self, fn, reads=(), writes=(), q="sp"):
        eng = q
        waits = []
        for b in reads:
            for t in b.w:
                self._need(eng, t, waits)
        for b in writes:
            if b.multi:
                continue
            for t in b.w:
                self._need(eng, t, waits)
            for t in b.r:
                self._need(eng, t, waits)
        k = self.dcnt[q]
        self.dcnt[q] += 1
        if k >= NDSEM:
            self._need(eng, ("d", q, k - NDSEM), waits)
        tok = ("d", q, k)
        for b in reads:
            b.r.append(tok)
        for b in writes:
            if b.multi:
                b.w.append(tok)
            else:
                b.w = [tok]
                b.r = []
        self.ops[eng].append((waits, fn, tok))
        self.nops += 1
        return tok

    def special(self, eng, name, fn, reads=(), writes=()):
        waits = []
        for b in reads:
            for t in b.w:
                self._need(eng, t, waits)
        tok = ("s", name, 0)
        self.specials.append(name)
        for b in writes:
            b.w = [tok]
            b.r = []
        self.ops[eng].append((waits, fn, tok))
        self.special_toks.append((eng, tok))
        return tok

    def final_wait(self, eng, bufs):
        waits = []
        for b in bufs:
            for t in b.w:
                self._need(eng, t, waits)
        self.ops[eng].append((waits, None, None))

    def _sem_for(self, tok):
        kind, e, i = tok
        if kind == "c":
            return self.sems[("c", e, (i - 1) % NSEM)], (i - 1) // NSEM + 1, 1
        if kind == "s":
            return self.sems[("s", e)], 1, 1
        return self.sems[("d", e, i % NDSEM)], 16 * (i // NDSEM + 1), 16

    def emit(self, stack):
        nc = self.nc
        for eng, tok in self.special_toks:
            self.ops[eng].append(([tok], None, None))
        for e in COMPUTE:
            for j in range(NSEM):
                self.sems[("c", e, j)] = stack.enter_context(nc.semaphore(f"s_{e}_{j}"))
        for q in self.dcnt:
            if self.dcnt[q] == 0:
                continue
            for j in range(NDSEM):
                self.sems[("d", q, j)] = stack.enter_context(nc.semaphore(f"d_{q}_{j}"))
        for nm in self.specials:
            self.sems[("s", nm)] = stack.enter_context(nc.semaphore(f"x_{nm}"))
        stack.enter_context(nc.allow_non_contiguous_dma(reason="tiny pad-column stores and strided weight tiles"))
        block = stack.enter_context(nc.Block())

        def run(eng_name):
            def body(engine):
                for waits, fn, tok in self.ops[eng_name]:
                    for w in waits:
                        sem, val, _ = self._sem_for(w)
                        engine.wait_ge(sem, val)
                    if fn is None:
                        continue
                    ins = fn(engine)
                    sem, val, inc = self._sem_for(tok)
                    ins.then_inc(sem, inc)
            return body

        block.tensor(run("pe"))
        block.scalar(run("act"))
        block.vector(run("dve"))
        block.gpsimd(run("pool"))
        block.sync(run("sp"))


class Arena:
    def __init__(self, nc, stack, words):
        self.t = stack.enter_context(nc.sbuf_tensor("arena", [128, words], F32))
        self.words = words
        self.top = 0
        self.peak = 0
        self.recs = []

    def mark(self):
        return self.top

    def release(self, m):
        self.top = m

    def alloc2(self, name, free_shape, dtype=F32):
        n = int(np.prod(free_shape))
        esz = 2 if dtype == BF16 else 4
        w = (n * esz + 3) // 4
        w = (w + 7) // 8 * 8
        off = self.top
        self.top += w
        self.peak = max(self.peak, self.top)
        assert self.top <= self.words, f"arena overflow at {name}: {self.top} > {self.words}"
        buf = Buf(name)
        keep = []
        inh = []
        for (s0, e0, b0) in self.recs:
            if s0 < off + w and off < e0:
                inh.extend(b0.w)
                inh.extend(b0.r)
                if not (s0 >= off and e0 <= off + w):
                    keep.append((s0, e0, b0))
            else:
                keep.append((s0, e0, b0))
        seen = set()
        for t in inh:
            if t not in seen:
                seen.add(t)
                buf.w.append(t)
        keep.append((off, off + w, buf))
        self.recs = keep
        ap = self.t[:, off:off + w]
        if dtype != F32:
            ap = ap.bitcast(dtype)
        ap = ap[:, 0:n]
        if len(free_shape) == 2:
            ap = ap.rearrange("p (a b) -> p a b", a=free_shape[0])
        elif len(free_shape) == 3:
            ap = ap.rearrange("p (a b c) -> p a b c", a=free_shape[0], b=free_shape[1])
        return ap, buf
D = 2048; KC = 16; T = 4096; TCX = 256; TALL = T + TCX; NOWN = 1024; NHALO = NOWN + 2
DFF = 5632
EPS = 1e-6
DECAY_SCALE = float(np.exp(-0.5))
LNX_EPS = 64e-5
ATTN_SCALE = float(192 ** -0.5)
NCOL_A = 2240
C_R, C_K, C_V, C_WL, C_AL, C_CQ, C_CKV, C_KPE = 0, 256, 512, 768, 960, 1152, 1664, 2176


class VecPack:
    def __init__(self):
        self.items = []
        self.off = {}
        self.n = 0

    def add(self, name, arr):
        arr = np.ascontiguousarray(arr, dtype=np.float32).reshape(128, -1)
        self.off[name] = (self.n, arr.shape[1])
        self.items.append(arr)
        self.n += arr.shape[1]

    def build(self):
        return np.concatenate(self.items, axis=1)


def fm(v):
    v = np.asarray(v, np.float32)
    return v.reshape(-1, 128).T

class KB:
    def __init__(self, nc, stack):
        self.nc = nc
        self.P = Prog(nc)
        self.st = stack
        self.A = Arena(nc, stack, 47104)
        self.banks = []
        for i in range(8):
            t = stack.enter_context(nc.psum_tensor(f"psb{i}", [128, 512], F32))
            self.banks.append((t, Buf(f"psb{i}")))
        self.bi = 0
        self.held = set()
        self.cast_i = 0

    def ps(self, hold=False):
        while (self.bi % 8) in self.held:
            self.bi += 1
        i = self.bi % 8
        t, b = self.banks[i]
        self.bi += 1
        if hold:
            self.held.add(i)
        return t, b

    def unhold(self, b):
        for i, (t_, b_) in enumerate(self.banks):
            if b_ is b:
                self.held.discard(i)

    def _e(self, eng):
        return "dve" if eng == "pool" else eng

    def mm(self, out, lhsT, rhs, start, stop, r, w):
        self.P.op("pe", lambda e: e.matmul(out, lhsT=lhsT, rhs=rhs, start=start, stop=stop), reads=r, writes=[w], pe_chain=True)

    def act(self, out, in_, func, r, w, bias=None, scale=1.0):
        if bias is None:
            self.P.op("act", lambda e: e.activation(out=out, in_=in_, func=func, scale=scale), reads=r, writes=w)
        else:
            self.P.op("act", lambda e: e.activation(out=out, in_=in_, func=func, bias=bias, scale=scale), reads=r, writes=w)

    def tt(self, out, a, b, op, r, w, eng="dve"):
        eng = self._e(eng)
        self.P.op(eng, lambda e: e.tensor_tensor(out=out, in0=a, in1=b, op=op), reads=r, writes=w)

    def ts(self, out, a, s1, s2, op0, op1, r, w, eng="dve"):
        eng = self._e(eng)
        if s2 is None:
            self.P.op(eng, lambda e: e.tensor_scalar(out=out, in0=a, scalar1=s1, scalar2=None, op0=op0), reads=r, writes=w)
        else:
            self.P.op(eng, lambda e: e.tensor_scalar(out=out, in0=a, scalar1=s1, scalar2=s2, op0=op0, op1=op1), reads=r, writes=w)

    def stt(self, out, a, s, b, op0, op1, r, w):
        self.P.op("dve", lambda e: e.scalar_tensor_tensor(out=out, in0=a, scalar=s, in1=b, op0=op0, op1=op1), reads=r, writes=w)

    def copy(self, out, in_, r, w, eng="dve"):
        eng = self._e(eng)
        if eng == "act":
            self.P.op("act", lambda e: e.activation(out=out, in_=in_, func=AF.Copy), reads=r, writes=w)
        else:
            self.P.op(eng, lambda e: e.tensor_copy(out=out, in_=in_), reads=r, writes=w)

    def memset(self, out, val, w, eng="pool"):
        eng = self._e(eng)
        self.P.op(eng, lambda e: e.memset(out, val), reads=[], writes=w)

    def recip(self, out, in_, r, w):
        self.P.op("dve", lambda e: e.reciprocal(out=out, in_=in_), reads=r, writes=w)

    def load(self, out, src, w, r=(), q="sp"):
        self.P.dma(lambda e: e.dma_start(out=out, in_=src), reads=list(r), writes=w, q=q)

    def store(self, dst, src, r, w, q="sp"):
        q = "sp"
        self.P.dma(lambda e: e.dma_start(out=dst, in_=src), reads=r, writes=w, q=q)

    def rsqrt(self, out, in_, mul, add, r, wbuf):
        self.ts(out, in_, mul, add, ALU.mult, ALU.add, r, [wbuf])
        self.act(out, out, AF.Sqrt, [wbuf], [wbuf])
        self.recip(out, out, [wbuf], [wbuf])


class WStream:
    def __init__(self, kb, nst=3, kcmax=16, ncmax=128):
        self.kb = kb
        self.nst = nst
        self.f = [kb.A.alloc2(f"wsf{i}", [kcmax, ncmax]) for i in range(nst)]
        self.b = [kb.A.alloc2(f"wsb{i}", [kcmax, ncmax], BF16) for i in range(nst)]
        self.i = 0

    def get(self, W, kc, blk, cast_eng=None):
        kb = self.kb
        ncols = 128
        i = self.i % self.nst
        self.i += 1
        fa, fb = self.f[i]
        ba, bb = self.b[i]
        src = W[blk].rearrange("p (c n) -> p c n", c=kc)
        kb.load(fa[:, 0:kc, 0:ncols], src, [fb])
        eng = cast_eng or ("act" if (self.i % 2) else "dve")
        kb.copy(ba[:, 0:kc, 0:ncols], fa[:, 0:kc, 0:ncols], [fb], [bb], eng=eng)
        return ba, bb

    def stream(self, reqs):
        pend = [self.get(*reqs[0])]
        for i in range(len(reqs)):
            if i + 1 < len(reqs):
                pend.append(self.get(*reqs[i + 1]))
            yield pend.pop(0)

def tok_blocks(n0, n, bs):
    out = []
    t = n0
    while t < n0 + n:
        m = min(bs, n0 + n - t)
        out.append((t, m))
        t += m
    return out


def build_program(nc, voff, nvec, dbg=None):
    st = ExitStack()
    kb = KB(nc, st)
    P, A = kb.P, kb.A

    shapes = {
        "xT_all": [D, TALL], "xT_own": [D, NHALO], "vec": [128, nvec], "w_ada": [D, 6 * D], "w_inA": [20, 128, D],
        "w_gate_in": [32, 128, D], "w2": [2, 96, 256], "a2": [2, 96, 256], "g2": [256, 256], "w_uq": [512, 1024],
        "w_ukvk": [512, 512], "w_ukvv": [512, 512], "w_rp": [16, 128, 1024], "w_mp": [16, 128, D], "w_out": [16, 128, D],
        "w_fg": [44, 128, D], "w_fv": [44, 128, D], "w_fd": [64, 128, 1408], "ident": [128, 128], "masks": [128, 7 * 512],
        "ropeC": [64, TALL], "ropeS": [64, TALL],
    }

    class LazyDram(dict):
        def __missing__(self, name):
            ap = nc.dram_tensor(name, list(shapes[name]), F32, kind="ExternalInput").ap()
            self[name] = ap
            return ap
    dram = LazyDram()
    xT_all = dram["xT_all"]; vec_d = dram["vec"]; w_ada = dram["w_ada"]; w_inA = dram["w_inA"]; ident_d = dram["ident"]
    oT = nc.dram_tensor("oT", [D, NOWN], F32, kind="ExternalOutput").ap()
    P1 = nc.dram_tensor("P1", [2560, TALL + 2], F32, kind="Internal").ap()
    XBs = [nc.dram_tensor(f"XBs{i}", [128, T], BF16, kind="Internal").ap() for i in range(6)]
    XBd = [nc.dram_tensor(f"XBd{i}", [512, T], BF16, kind="Internal").ap() for i in range(6)]
    X1 = nc.dram_tensor("X1", [D, NHALO], F32, kind="Internal").ap()
    bP1 = Buf("P1", True); bXBs = [Buf(f"XBs{i}", True) for i in range(6)]; bXBd = [Buf(f"XBd{i}") for i in range(6)]; bX1 = Buf("X1", True); bOT = Buf("oT", True)
    dbg_out = None
    if dbg is not None:
        dbg_out = nc.dram_tensor("dbg", list(dbg), F32, kind="ExternalOutput").ap()

    vec, bvec = A.alloc2("vec", [nvec])
    kb.load(vec, vec_d, [bvec])

    def V(name, i=0, n=1):
        o, w = voff[name]
        return vec[:, o + i:o + i + n]

    ones_bf, bones = A.alloc2("ones_bf", [128], BF16)
    kb.memset(ones_bf, 1.0, [bones])
    ident, bident = A.alloc2("ident", [128])
    kb.load(ident, ident_d, [bident])
    der, bder = A.alloc2("der", [16 * 10])

    def DV(k, c):
        return der[:, k * 16 + c:k * 16 + c + 1]
    A1, B1, A1C, B1C, A2, B2, G1, G2 = range(8)
    mark0 = A.mark()

    sT, bsT = A.alloc2("sT", [32])
    modv, bmod = A.alloc2("modv", [192])
    o, _ = voff["cT"]
    kb.act(sT, vec[:, o:o + 32], AF.Sigmoid, [bvec], [bsT])
    kb.tt(sT, sT, vec[:, o:o + 32], ALU.mult, [bsT, bvec], [bsT])
    m = A.mark()
    wa = [A.alloc2(f"wada{i}", [16, 512]) for i in range(2)]
    pm, bpm = kb.ps(hold=True)
    for nb in range(24):
        wt, wb = wa[nb % 2]
        kb.load(wt, w_ada[:, nb * 512:(nb + 1) * 512].rearrange("(c p) n -> p c n", p=128), [wb])
        for j in range(4):
            nn = nb * 4 + j
            for kc in range(16):
                kb.mm(pm[:, nn * 2:nn * 2 + 2], wt[:, kc, j * 128:(j + 1) * 128], sT[:, kc * 2:kc * 2 + 2], kc == 0, kc == 15, [wb, bsT], bpm)
    ob, _ = voff["b_ada"]
    mv3 = modv.rearrange("p (n r) -> p n r", r=2)
    pm3 = pm[:, 0:192].rearrange("p (n r) -> p n r", r=2)
    bb3 = vec[:, ob:ob + 96].rearrange("p (n r) -> p n r", r=1).to_broadcast([128, 96, 2])
    kb.tt(mv3, pm3, bb3, ALU.add, [bpm, bvec], [bmod])
    kb.unhold(bpm)
    A.release(m)

    def MOD(j6, row):
        return mv3[:, j6 * 16:(j6 + 1) * 16, row]
    d3 = der.rearrange("p (k c) -> p k c", c=16)
    onp, _ = voff["npre"]; onq, _ = voff["npost"]; onf, _ = voff["nfpre"]; ong, _ = voff["nfpost"]
    kb.stt(d3[:, A1, :], MOD(1, 0), 1.0, vec[:, onp:onp + 16], ALU.add, ALU.mult, [bmod, bvec], [bder])
    kb.copy(d3[:, B1, :], MOD(0, 0), [bmod], [bder])
    kb.stt(d3[:, A1C, :], MOD(1, 1), 1.0, vec[:, onp:onp + 16], ALU.add, ALU.mult, [bmod, bvec], [bder])
    kb.copy(d3[:, B1C, :], MOD(0, 1), [bmod], [bder])
    kb.stt(d3[:, A2, :], MOD(4, 0), 1.0, vec[:, onf:onf + 16], ALU.add, ALU.mult, [bmod, bvec], [bder])
    kb.copy(d3[:, B2, :], MOD(3, 0), [bmod], [bder])
    kb.tt(d3[:, G1, :], MOD(2, 0), vec[:, onq:onq + 16], ALU.mult, [bmod, bvec], [bder])
    kb.tt(d3[:, G2, :], MOD(5, 0), vec[:, ong:ong + 16], ALU.mult, [bmod, bvec], [bder])

    cxe = dict(kb=kb, P=P, A=A, st=st, P1=P1, bP1=bP1, der=der, bder=bder)
    if os.environ.get("PH0_STOP") == "0":
        return cxe
    def norm_mod(xb, bxb, n, hout, bh, sqb, bsq, rs, brs, Ak, Bk):
        kb.act(sqb[:, :, 0:n], xb[:, :, 0:n], AF.Square, [bxb], [bsq])
        pp, bpp = kb.ps()
        for c in range(16):
            kb.mm(pp[:, 0:n], ones_bf, sqb[:, c, 0:n], c == 0, c == 15, [bones, bsq], bpp)
        kb.rsqrt(rs[:, 0:n], pp[:, 0:n], 1.0 / D, EPS, [bpp], brs)
        for c in range(16):
            kb.stt(xb[:, c, 0:n], xb[:, c, 0:n], DV(Ak, c), rs[:, 0:n], ALU.mult, ALU.mult, [bxb, bder, brs], [bxb])
            kb.act(hout[:, c, :], xb[:, c, 0:n], AF.Identity, [bxb, bder], [bh], bias=DV(Bk, c))

    hT, bhT = A.alloc2("hT_all", [16, TALL], BF16)
    m = A.mark()
    xbs = [A.alloc2(f"xb{i}", [16, 256]) for i in range(2)]
    sqb, bsq = A.alloc2("sqb", [16, 256], BF16)
    rs, brs = A.alloc2("rs", [256])
    xall3 = xT_all.rearrange("(c p) t -> p c t", p=128)
    for bi, (t0, n) in enumerate(tok_blocks(0, TALL, 256)):
        xb, bxb = xbs[bi % 2]
        kb.load(xb[:, :, 0:n], xall3[:, :, t0:t0 + n], [bxb])
        isctx = t0 < TCX
        norm_mod(xb, bxb, n, hT[:, :, t0:t0 + n], bhT, sqb, bsq, rs, brs, A1C if isctx else A1, B1C if isctx else B1)
    A.release(m)

    if os.environ.get("PH0_STOP") == "1":
        return cxe
    m = A.mark()
    ws = WStream(kb, nst=3)
    stg = [A.alloc2(f"stg{i}", [512]) for i in range(3)]
    si = 0
    blocks = [(0, 256)] + tok_blocks(256, T, 512)
    zp, bzp = A.alloc2("zp", [2])
    kb.memset(zp, 0.0, [bzp])
    _v = os.environ.get("P2VAR", "")
    for ci in range(20):
        if "nopad" in _v:
            break
        kb.store(P1[ci * 128:(ci + 1) * 128, 0:1], zp[:, 0:1], [bzp], [bP1])
        kb.store(P1[ci * 128:(ci + 1) * 128, TALL + 1:TALL + 2], zp[:, 1:2], [bzp], [bP1])
    for ci, (wbf, bw) in enumerate(ws.stream([(w_inA, 16, ci) for ci in range(20)])):
        for (t0, n) in blocks:
            pp, bpp = kb.ps()
            for c in range(16):
                kb.mm(pp[:, 0:n], wbf[:, c, :], hT[:, c, t0:t0 + n], c == 0, c == 15, [bw, bhT], bpp)
            sg, bsg = stg[si % 3]
            kb.copy(sg[:, 0:n], pp[:, 0:n], [bpp], [bsg], eng=("act" if si % 2 else "dve"))
            si += 1
            kb.store(P1[ci * 128:(ci + 1) * 128, 1 + t0:1 + t0 + n], sg[:, 0:n], [bsg], [bP1], q=("sp" if "spstore" in _v else "pool"))
    A.release(m)
    A.release(mark0)
    ctx = dict(kb=kb, P=P, A=A, V=V, vec=vec, bvec=bvec, DV=DV, der=der, bder=bder, ident=ident, bident=bident,
               ones_bf=ones_bf, bones=bones, P1=P1, bP1=bP1, XBs=XBs, bXBs=bXBs, XBd=XBd, bXBd=bXBd, X1=X1, bX1=bX1,
               oT=oT, bOT=bOT, dram=dram, norm_mod=norm_mod, voff=voff, st=st, dbg_out=dbg_out,
               consts=(A1, B1, A1C, B1C, A2, B2, G1, G2))
    return ctx

def phase3_rwkv(cx):
    kb, P, A, V, vec, bvec = cx["kb"], cx["P"], cx["A"], cx["V"], cx["vec"], cx["bvec"]
    P1, bP1, XBs, bXBs = cx["P1"], cx["bP1"], cx["XBs"], cx["bXBs"]
    ident, bident = cx["ident"], cx["bident"]
    dram, voff = cx["dram"], cx["voff"]
    m_phase = A.mark()

    def T_(name, shape, dt=F32):
        return A.alloc2(name, shape, dt)

    msk, bmsk = T_("masks", [7 * 512])
    kb.load(msk, dram["masks"], [bmsk])
    SU, SL, IU, IL, I8 = [msk[:, i * 512:(i + 1) * 512] for i in range(5)]
    onesblk = msk[:, 5 * 512:5 * 512 + 128]
    om, bom = T_("om", [12]); hm, bhm = T_("hm", [12]); omka, bomka = T_("omka", [2])
    omu, _ = voff["mus"]
    kb.ts(om, vec[:, omu:omu + 12], -1.0, 1.0, ALU.mult, ALU.add, [bvec], [bom])
    kb.ts(hm, vec[:, omu:omu + 12], 0.5, None, ALU.mult, None, [bvec], [bhm])
    oka, _ = voff["k_a"]
    kb.ts(omka, vec[:, oka:oka + 2], -1.0, 1.0, ALU.mult, ALU.add, [bvec], [bomka])
    w2f, bw2f = T_("w2f", [2, 256]); a2f, ba2f = T_("a2f", [2, 256]); g2f, bg2f = T_("g2f", [2, 256])
    w2b, bw2b = T_("w2b", [2, 256], BF16); a2b, ba2b = T_("a2b", [2, 256], BF16); g2b, bg2b = T_("g2b", [2, 256], BF16)
    kb.load(w2f[0:96], dram["w2"].rearrange("d r c -> r d c"), [bw2f])
    kb.load(a2f[0:96], dram["a2"].rearrange("d r c -> r d c"), [ba2f])
    kb.load(g2f, dram["g2"].rearrange("(c p) n -> p c n", p=128), [bg2f])
    kb.copy(w2b[0:96], w2f[0:96], [bw2f], [bw2b]); kb.copy(a2b[0:96], a2f[0:96], [ba2f], [ba2b]); kb.copy(g2b, g2f, [bg2f], [bg2b])

    def shift_lerp(Z, bZ, n, out, bout, mi, col, tmp, btmp, parts=128):
        p = slice(0, parts)
        kb.tt(tmp[p, 0:n], Z[p, 0:n], Z[p, 2:n + 2], ALU.add, [bZ], [btmp], eng="pool")
        kb.ts(tmp[p, 0:n], tmp[p, 0:n], hm[p, mi * 2 + col:mi * 2 + col + 1], None, ALU.mult, None, [btmp, bhm], [btmp])
        kb.stt(out, Z[p, 1:n + 1], om[p, mi * 2 + col:mi * 2 + col + 1], tmp[p, 0:n], ALU.mult, ALU.add, [bZ, bom, btmp], [bout])

    def load_halo(Z, bZ, row0, nrows, t0, n, left_edge, right_edge):
        kb.load(Z[0:nrows, 0:n + 2], P1[row0:row0 + nrows, t0:t0 + n + 2], [bZ], r=[bP1])
        if left_edge:
            kb.memset(Z[0:nrows, 0:1], 0.0, [bZ])
        if right_edge:
            kb.memset(Z[0:nrows, n + 1:n + 2], 0.0, [bZ])

    NW = 256
    NCH = NW // 64

    class SubArena:
        def __init__(self, base, size):
            self.base, self.size, self.top = base, size, base

        def reset(self):
            self.top = self.base

    def window_gen(hp, d, t0, n, Hst, bH, SA, Yacc, bY, Bacc, bB, Gt, bG):
        def T_(name, shape, dt=F32):
            save = A.top
            A.top = SA.top
            r = A.alloc2(name, shape, dt)
            assert A.top <= SA.base + SA.size, f"subarena overflow {name} {A.top - SA.base} > {SA.size}"
            SA.top = A.top
            A.top = save
            return r
        MS, MST, MI = (SU, SL, IU) if d == 0 else (SL, SU, IL)
        isctx = t0 < TCX
        nch = n // 64
        le = t0 in (0, TCX); re = (t0 + n) in (TCX, TALL)
        Zr, bZr = T_("Zr", [NW + 2]); Zk, bZk = T_("Zk", [NW + 2]); Zv, bZv = T_("Zv", [NW + 2])
        Zw, bZw = T_("Zw", [NW + 2]); Za, bZa = T_("Za", [NW + 2])
        tmp, btmp = T_("tmp", [NW]); tmp2, btmp2 = T_("tmp2", [NW])
        r_s, br = T_("r_s", [NW]); k_s, bk = T_("k_s", [NW]); VT, bVT = T_("VT", [64 + NW])
        load_halo(Zr, bZr, 0 + hp * 128, 128, t0, n, le, re)
        load_halo(Zk, bZk, 256 + hp * 128, 128, t0, n, le, re)
        load_halo(Zv, bZv, 512 + hp * 128, 128, t0, n, le, re)
        load_halo(Zw, bZw, 768 + d * 96, 96, t0, n, le, re)
        load_halo(Za, bZa, 960 + d * 96, 96, t0, n, le, re)
        shift_lerp(Zr, bZr, n, r_s[:, 0:n], br, 0, hp, tmp, btmp)
        shift_lerp(Zk, bZk, n, k_s[:, 0:n], bk, 1, hp, tmp, btmp)
        kb.memset(VT[:, 0:64], 0.0, [bVT])
        shift_lerp(Zv, bZv, n, VT[:, 64:64 + n], bVT, 2, hp, tmp, btmp)
        v_s = VT[:, 64:64 + NW]
        wl_s, bwl = T_("wl_s", [NW]); twl, btwl = T_("twl", [NW], BF16); alb, balb = T_("alb", [NW], BF16)
        shift_lerp(Zw, bZw, n, wl_s[0:96, 0:n], bwl, 3, d, tmp, btmp, parts=96)
        kb.act(twl[0:96, 0:n], wl_s[0:96, 0:n], AF.Tanh, [bwl], [btwl])
        shift_lerp(Za, bZa, n, wl_s[0:96, 0:n], bwl, 4, d, tmp, btmp, parts=96)
        kb.copy(alb[0:96, 0:n], wl_s[0:96, 0:n], [bwl], [balb], eng="pool")
        yield
        sg, bsg = T_("sg", [NW]); asig, bas = T_("asig", [NW])
        pw, bpw = kb.ps(); pa, bpa = kb.ps()
        kb.mm(pw[:, 0:n], w2b[0:96, d, hp * 128:(hp + 1) * 128], twl[0:96, 0:n], True, True, [bw2b, btwl], bpw)
        kb.mm(pa[:, 0:n], a2b[0:96, d, hp * 128:(hp + 1) * 128], alb[0:96, 0:n], True, True, [ba2b, balb], bpa)
        kb.act(sg[:, 0:n], pw[:, 0:n], AF.Sigmoid, [bpw, bvec], [bsg], bias=V("w0", d * 2 + hp))
        kb.act(asig[:, 0:n], pa[:, 0:n], AF.Sigmoid, [bpa, bvec], [bas], bias=V("a0", d * 2 + hp))
        kk, bkk = T_("kk", [NW]); kd, bkd = T_("kd", [NW]); bb_, bbb = T_("b", [NW])
        kb.ts(kk[:, 0:n], k_s[:, 0:n], V("k_k", hp), None, ALU.mult, None, [bk, bvec], [bkk])
        kb.tt(tmp[:, 0:n], kk[:, 0:n], kk[:, 0:n], ALU.mult, [bkk], [btmp], eng="pool")
        pk, bpk = kb.ps()
        kb.mm(pk[:, 0:n], onesblk, tmp[:, 0:n], True, True, [bmsk, btmp], bpk)
        kb.ts(tmp2[:, 0:n], pk[:, 0:n], 1e-24, None, ALU.max, None, [bpk], [btmp2])
        kb.act(tmp2[:, 0:n], tmp2[:, 0:n], AF.Sqrt, [btmp2], [btmp2])
        kb.recip(tmp2[:, 0:n], tmp2[:, 0:n], [btmp2], [btmp2])
        kb.tt(kk[:, 0:n], kk[:, 0:n], tmp2[:, 0:n], ALU.mult, [bkk, btmp2], [bkk])
        yield
        kb.ts(tmp[:, 0:n], asig[:, 0:n], V("k_a", hp), omka[:, hp:hp + 1], ALU.mult, ALU.add, [bas, bvec, bomka], [btmp])
        kb.tt(kd[:, 0:n], k_s[:, 0:n], tmp[:, 0:n], ALU.mult, [bk, btmp], [bkd])
        kb.tt(bb_[:, 0:n], kk[:, 0:n], asig[:, 0:n], ALU.mult, [bkk, bas], [bbb], eng="pool")
        if not isctx:
            tl = t0 - TCX
            kb.stt(tmp[:, 0:n], r_s[:, 0:n], V("r_k", hp), kd[:, 0:n], ALU.mult, ALU.mult, [br, bvec, bkd], [btmp])
            pb, bpb = kb.ps()
            kb.mm(pb[:, 0:n], onesblk, tmp[:, 0:n], True, True, [bmsk, btmp], bpb)
            kb.stt(tmp2[:, 0:n], pb[:, 0:n], 0.5, v_s[:, 0:n], ALU.mult, ALU.mult, [bpb, bVT], [btmp2])
            kb.tt(Bacc[:, tl:tl + n], Bacc[:, tl:tl + n], tmp2[:, 0:n], ALU.add, [bB, btmp2], [bB], eng="pool")
            if d == 0:
                Zg, bZg = T_("Zg", [2, NW + 2]); glb, bglb = T_("glb", [2, NW], BF16)
                for c in range(2):
                    kb.load(Zg[:, c, 0:n + 2], P1[1152 + c * 128:1152 + (c + 1) * 128, t0:t0 + n + 2], [bZg], r=[bP1])
                if le:
                    kb.memset(Zg[:, :, 0:1], 0.0, [bZg])
                if re:
                    kb.memset(Zg[:, :, n + 1:n + 2], 0.0, [bZg])
                for c in range(2):
                    shift_lerp(Zg[:, c, :], bZg, n, tmp2[:, 0:n], btmp2, 5, c, tmp, btmp)
                    kb.act(glb[:, c, 0:n], tmp2[:, 0:n], AF.Sigmoid, [btmp2], [bglb])
                pg, bpg = kb.ps()
                for c in range(2):
                    kb.mm(pg[:, 0:n], g2b[:, c, hp * 128:(hp + 1) * 128], glb[:, c, 0:n], c == 0, c == 1, [bg2b, bglb], bpg)
                kb.copy(Gt[:, tl:tl + n], pg[:, 0:n], [bpg], [bG], eng="act")
        yield
        csA, bcA = T_("csA", [NW]); csB, bcB = T_("csB", [NW])
        src, bsrc = sg, bsg
        dsts = [(csA, bcA), (csB, bcB)]
        for si_, s in enumerate((1, 2, 4, 8, 16, 32)):
            dst, bdst = dsts[si_ % 2]
            s3 = src[:, 0:n].rearrange("p (c t) -> p c t", t=64)
            d3 = dst[:, 0:n].rearrange("p (c t) -> p c t", t=64)
            if d == 0:
                kb.tt(d3[:, :, s:], s3[:, :, s:], s3[:, :, :64 - s], ALU.add, [bsrc], [bdst])
                kb.copy(d3[:, :, :s], s3[:, :, :s], [bsrc], [bdst], eng="pool")
            else:
                kb.tt(d3[:, :, :64 - s], s3[:, :, :64 - s], s3[:, :, s:], ALU.add, [bsrc], [bdst])
                kb.copy(d3[:, :, 64 - s:], s3[:, :, 64 - s:], [bsrc], [bdst], eng="pool")
            src, bsrc = dst, bdst
        cs, bcs = src, bsrc
        cs3 = cs[:, 0:n].rearrange("p (c t) -> p c t", t=64)
        endc = 63 if d == 0 else 0
        E1, bE1 = T_("E1", [NW]); E2, bE2 = T_("E2", [NW]); E3, bE3 = T_("E3", [NW]); E4, bE4 = T_("E4", [NW])
        kb.act(E1[:, 0:n], cs[:, 0:n], AF.Exp, [bcs], [bE1], scale=-DECAY_SCALE)
        kb.act(E2[:, 0:n], cs[:, 0:n], AF.Exp, [bcs], [bE2], scale=DECAY_SCALE)
        kb.tt(tmp[:, 0:n], cs[:, 0:n], sg[:, 0:n], ALU.subtract, [bcs, bsg], [btmp], eng="pool")
        kb.act(E3[:, 0:n], tmp[:, 0:n], AF.Exp, [btmp], [bE3], scale=-DECAY_SCALE)
        t3 = tmp2[:, 0:n].rearrange("p (c t) -> p c t", t=64)
        kb.tt(t3, cs3, cs3[:, :, endc:endc + 1].to_broadcast([128, nch, 64]), ALU.subtract, [bcs], [btmp2])
        kb.act(E4[:, 0:n], tmp2[:, 0:n], AF.Exp, [btmp2], [bE4], scale=DECAY_SCALE)
        yield
        pC = E1[:, 0:n].rearrange("p (c t) -> p c t", t=64)
        RT, bRT = T_("RT", [NW]); AT, bAT = T_("AT", [NW]); BKT, bBKT = T_("BKT", [NCH, 128]); BKpT, bBKpT = T_("BKpT", [NCH, 128])
        kb.tt(RT[:, 0:n], r_s[:, 0:n], E1[:, 0:n], ALU.mult, [br, bE1], [bRT], eng="pool")
        kb.stt(AT[:, 0:n], kk[:, 0:n], -1.0, E3[:, 0:n], ALU.mult, ALU.mult, [bkk, bE3], [bAT])
        b3 = bb_[:, 0:n].rearrange("p (c t) -> p c t", t=64); kd3 = kd[:, 0:n].rearrange("p (c t) -> p c t", t=64)
        e23 = E2[:, 0:n].rearrange("p (c t) -> p c t", t=64); e43 = E4[:, 0:n].rearrange("p (c t) -> p c t", t=64)
        kb.tt(BKT[:, 0:nch, 0:64], b3, e23, ALU.mult, [bbb, bE2], [bBKT])
        kb.tt(BKT[:, 0:nch, 64:128], kd3, e23, ALU.mult, [bkd, bE2], [bBKT], eng="pool")
        kb.tt(BKpT[:, 0:nch, 0:64], b3, e43, ALU.mult, [bbb, bE4], [bBKpT])
        kb.tt(BKpT[:, 0:nch, 64:128], kd3, e43, ALU.mult, [bkd, bE4], [bBKpT], eng="pool")

        def chunk(tile_, j):
            return tile_[:, j * 64:(j + 1) * 64]
        yield
        W = nch * 64

        def PT(name):
            return T_(name, [NCH, 64])
        LAK, bLAK = PT("LAK"); LRu, bLRu = PT("LRu"); LRv, bLRv = PT("LRv")
        Bp, bBp = PT("Bp"); Kp, bKp = PT("Kp"); Vtm, bVtm = PT("Vtm"); Utm, bUtm = PT("Utm")
        Xa, bXa = PT("Xa"); XTa, bXTa = PT("XTa"); Xb, bXb = PT("Xb"); XTb, bXTb = PT("XTb")
        Qa, bQa = PT("Qa"); Qb, bQb = PT("Qb")

        def fl(t_):
            return t_.rearrange("p c t -> p (c t)")[:, 0:W]

        def packed(lhs_fn, rhs_fn, rbufs):
            pp, bpp = kb.ps()
            for hh in range(2):
                h_ = slice(hh * 64, hh * 64 + 64)
                for j in range(nch):
                    kb.mm(pp[h_, j * 64:(j + 1) * 64], lhs_fn(h_, j), rhs_fn(h_, j), True, True, rbufs, bpp)
            return pp, bpp
        AT_c = lambda h_, j: chunk(AT, j)[h_]
        RT_c = lambda h_, j: chunk(RT, j)[h_]
        idh = lambda h_, j: ident[h_, h_]
        pN, bpN = packed(lambda h_, j: BKT[h_, j, 0:64], AT_c, [bBKT, bAT])
        kb.tt(fl(Xa), pN[:, 0:W], MS[:, 0:W], ALU.mult, [bpN, bmsk], [bXa])
        pNT, bpNT = packed(AT_c, lambda h_, j: BKT[h_, j, 0:64], [bBKT, bAT])
        kb.tt(fl(XTa), pNT[:, 0:W], MST[:, 0:W], ALU.mult, [bpNT, bmsk], [bXTa])
        pp, bpp = packed(lambda h_, j: BKT[h_, j, 64:128], AT_c, [bBKT, bAT])
        kb.tt(fl(LAK), pp[:, 0:W], MS[:, 0:W], ALU.mult, [bpp, bmsk], [bLAK])
        yield
        if not isctx:
            pp, bpp = packed(lambda h_, j: BKT[h_, j, 0:64], RT_c, [bBKT, bRT])
            kb.tt(fl(LRu), pp[:, 0:W], MI[:, 0:W], ALU.mult, [bpp, bmsk], [bLRu])
            pp, bpp = packed(lambda h_, j: BKT[h_, j, 64:128], RT_c, [bBKT, bRT])
            kb.tt(fl(LRv), pp[:, 0:W], MI[:, 0:W], ALU.mult, [bpp, bmsk], [bLRv])
        pp, bpp = packed(lambda h_, j: BKpT[h_, j, 0:64], idh, [bBKpT, bident])
        kb.copy(fl(Bp), pp[:, 0:W], [bpp], [bBp], eng="act")
        pp, bpp = packed(lambda h_, j: BKpT[h_, j, 64:128], idh, [bBKpT, bident])
        kb.copy(fl(Kp), pp[:, 0:W], [bpp], [bKp], eng="act")
        pp, bpp = packed(lambda h_, j: VT[h_, 64 + j * 64:64 + (j + 1) * 64], idh, [bVT, bident])
        kb.copy(fl(Vtm), pp[:, 0:W], [bpp], [bVtm], eng="act")
        yield
        kb.tt(fl(Qa), fl(Xa), I8[:, 0:W], ALU.add, [bXa, bmsk], [bQa], eng="pool")
        X, bX, XT, bXT, Q, bQ = Xa, bXa, XTa, bXTa, Qa, bQa
        Xn, bXn, XTn, bXTn, Qn, bQn = Xb, bXb, XTb, bXTb, Qb, bQb
        for lev in range(1, 6):
            pB, bpB = packed(lambda h_, j: X[h_, j, :], lambda h_, j: XT[h_, j, :], [bX, bXT])
            kb.copy(fl(XTn), pB[:, 0:W], [bpB], [bXTn], eng="act")
            if lev < 5:
                pA, bpA = packed(lambda h_, j: XT[h_, j, :], lambda h_, j: X[h_, j, :], [bX, bXT])
                kb.copy(fl(Xn), pA[:, 0:W], [bpA], [bXn], eng="dve")
            pQ, bpQ = packed(lambda h_, j: XTn[h_, j, :], lambda h_, j: Q[h_, j, :], [bXTn, bQ])
            kb.tt(fl(Qn), pQ[:, 0:W], fl(Q), ALU.add, [bpQ, bQ], [bQn])
            X, bX, Xn, bXn = Xn, bXn, X, bX
            yield
            XT, bXT, XTn, bXTn = XTn, bXTn, XT, bXT
            Q, bQ, Qn, bQn = Qn, bQn, Q, bQ
        TT, bTT = Q, bQ
        yield
        Wsb, bWsb = T_("Wsb", [64])
        pY, bpY = (None, None) if isctx else kb.ps(hold=True)
        jorder = range(nch) if d == 0 else range(nch - 1, -1, -1)
        for j in jorder:
            pW, bpW = kb.ps()
            for hh in range(2):
                h_ = slice(hh * 64, hh * 64 + 64)
                kb.mm(pW[h_, 0:64], chunk(AT, j)[h_], Hst[h_, :], True, False, [bAT, bH], bpW)
                kb.mm(pW[h_, 0:64], LAK[h_, j, :], Vtm[h_, j, :], False, True, [bLAK, bVtm], bpW)
            kb.copy(Wsb, pW[:, 0:64], [bpW], [bWsb], eng="act")
            pU, bpU = kb.ps()
            for hh in range(2):
                h_ = slice(hh * 64, hh * 64 + 64)
                kb.mm(pU[h_, 0:64], TT[h_, j, :], Wsb[h_, :], True, True, [bTT, bWsb], bpU)
            kb.copy(Utm[:, j, :], pU[:, 0:64], [bpU], [bUtm], eng="dve")
            if not isctx:
                for hh in range(2):
                    h_ = slice(hh * 64, hh * 64 + 64)
                    o_ = pY[h_, j * 64:(j + 1) * 64]
                    kb.mm(o_, Hst[h_, :], chunk(RT, j)[h_], True, False, [bH, bRT], bpY)
                    kb.mm(o_, Utm[h_, j, :], LRu[h_, j, :], False, False, [bUtm, bLRu], bpY)
                    kb.mm(o_, Vtm[h_, j, :], LRv[h_, j, :], False, True, [bVtm, bLRv], bpY)
            pH, bpH = kb.ps()
            for hh in range(2):
                h_ = slice(hh * 64, hh * 64 + 64)
                kb.mm(pH[h_, 0:64], Bp[h_, j, :], Utm[h_, j, :], True, False, [bBp, bUtm], bpH)
                kb.mm(pH[h_, 0:64], Kp[h_, j, :], Vtm[h_, j, :], False, True, [bKp, bVtm], bpH)
            kb.stt(Hst, Hst, pC[:, j, endc:endc + 1], pH[:, 0:64], ALU.mult, ALU.add, [bH, bE1, bpH], [bH])
            yield
        if not isctx:
            tl = t0 - TCX
            kb.tt(Yacc[:, tl:tl + n], Yacc[:, tl:tl + n], pY[:, 0:n], ALU.add, [bY, bpY], [bY])
            kb.unhold(bpY)
        yield

    def stream_gen(hp, d, SA, Yacc, bY, Bacc, bB, Gt, bG):
        SA.reset()
        save = A.top; A.top = SA.top
        Hst, bH = A.alloc2("Hst", [64]); SA.top = A.top; A.top = save
        hbase = SA.top
        kb.memset(Hst, 0.0, [bH])
        order = windows if d == 0 else [windows[0]] + windows[:0:-1]
        for (t0, n) in order:
            SA.top = hbase
            yield from window_gen(hp, d, t0, n, Hst, bH, SA, Yacc, bY, Bacc, bB, Gt, bG)

    windows = [(0, 256)] + [(256 + NW * i, NW) for i in range(T // NW)]
    for hp in range(2):
        m_hp = A.mark()
        Yacc, bY = T_("Yacc", [T]); Bacc, bB = T_("Bacc", [T]); Gt, bG = T_("Gt", [T])
        kb.memset(Yacc, 0.0, [bY]); kb.memset(Bacc, 0.0, [bB])
        rem = A.words - A.top
        half = (rem // 2) // 8 * 8
        SAs = [SubArena(A.top, half), SubArena(A.top + half, half)]
        gens = [stream_gen(hp, d, SAs[d], Yacc, bY, Bacc, bB, Gt, bG) for d in range(2)]
        while gens:
            for g_ in list(gens):
                try:
                    next(g_)
                except StopIteration:
                    gens.remove(g_)
        A.peak = max(A.peak, A.top + 2 * half)

        m_f = A.mark()
        yc, byc = T_("yc", [512]); sq, bsq = T_("sq", [512]); rsd, brsd = T_("rsd", [512]); yo, byo = T_("yo", [512], BF16)
        for (t0, n) in tok_blocks(0, T, 512):
            pm_, bpm_ = kb.ps()
            kb.mm(pm_[:, 0:n], onesblk, Yacc[:, t0:t0 + n], True, True, [bmsk, bY], bpm_)
            kb.stt(yc, pm_[:, 0:n], -1.0 / 64, Yacc[:, t0:t0 + n], ALU.mult, ALU.add, [bpm_, bY], [byc])
            kb.tt(sq, yc, yc, ALU.mult, [byc], [bsq], eng="pool")
            pv_, bpv_ = kb.ps()
            kb.mm(pv_[:, 0:n], onesblk, sq, True, True, [bmsk, bsq], bpv_)
            kb.rsqrt(rsd, pv_[:, 0:n], 1.0 / 64, LNX_EPS, [bpv_], brsd)
            kb.tt(yc, yc, rsd, ALU.mult, [byc, brsd], [byc])
            kb.ts(yc, yc, V("lnx_w", hp), V("lnx_b", hp), ALU.mult, ALU.add, [byc, bvec], [byc])
            kb.tt(yc, yc, Bacc[:, t0:t0 + n], ALU.add, [byc, bB], [byc], eng="pool")
            kb.tt(yo, yc, Gt[:, t0:t0 + n], ALU.mult, [byc, bG], [byo])
            kb.store(XBs[hp][:, t0:t0 + n], yo, [byo], [bXBs[hp]])
        A.release(m_f)
        A.release(m_hp)
    A.release(m_phase)

def phase4_mla(cx):
    kb, P, A, V, vec, bvec = cx["kb"], cx["P"], cx["A"], cx["V"], cx["vec"], cx["bvec"]
    P1, bP1, XBs, bXBs = cx["P1"], cx["bP1"], cx["XBs"], cx["bXBs"]
    ones_bf, bones = cx["ones_bf"], cx["bones"]
    dram = cx["dram"]
    m_phase = A.mark()

    def T_(name, shape, dt=F32):
        return A.alloc2(name, shape, dt)
    C_CQ, C_CKV, C_KPE, C_KPESW = 1408, 1920, 2432, 2496
    Kn, bKn = T_("Kn", [4, TALL], BF16); Kr, bKr = T_("Kr", [TALL], BF16); Vt, bVt = T_("Vt", [34, 512], BF16)
    kb.memset(Kr[64:128, :], 0.0, [bKr])
    wq, bwq = T_("wq", [4, 1024], BF16); wkk, bwkk = T_("wkk", [4, 512], BF16); wkv, bwkv = T_("wkv", [4, 512], BF16)
    m = A.mark()
    wf, bwf = T_("wf", [4, 1024])
    kb.load(wf, dram["w_uq"].rearrange("(c p) n -> p c n", p=128), [bwf]); kb.copy(wq, wf, [bwf], [bwq], eng="pool")
    kb.load(wf[:, :, 0:512], dram["w_ukvk"].rearrange("(c p) n -> p c n", p=128), [bwf]); kb.copy(wkk, wf[:, :, 0:512], [bwf], [bwkk], eng="pool")
    kb.load(wf[:, :, 0:512], dram["w_ukvv"].rearrange("(c p) n -> p c n", p=128), [bwf]); kb.copy(wkv, wf[:, :, 0:512], [bwf], [bwkv], eng="pool")
    A.release(m)
    xin, bxin = T_("xin", [4, 512]); sqb, bsq = T_("sqb4", [4, 512], BF16); rs, brs = T_("rs4", [512]); cn, bcn = T_("cn", [4, 512], BF16)
    kp, bkp = T_("kp", [512]); kps, bkps = T_("kps", [512]); rc, brc = T_("rc", [512]); rsn, brsn = T_("rsn", [512])
    t1, bt1 = T_("t1", [512]); t2, bt2 = T_("t2", [512])

    def rms4(row0, t0, n, gname):
        kb.load(xin[:, :, 0:n], P1[row0:row0 + 512, 1 + t0:1 + t0 + n].rearrange("(c p) t -> p c t", p=128), [bxin], r=[bP1])
        kb.act(sqb[:, :, 0:n], xin[:, :, 0:n], AF.Square, [bxin], [bsq])
        pp, bpp = kb.ps()
        for c in range(4):
            kb.mm(pp[:, 0:n], ones_bf, sqb[:, c, 0:n], c == 0, c == 3, [bones, bsq], bpp)
        kb.rsqrt(rs[:, 0:n], pp[:, 0:n], 1.0 / 512, EPS, [bpp], brs)
        for c in range(4):
            kb.stt(cn[:, c, 0:n], xin[:, c, 0:n], V(gname, c), rs[:, 0:n], ALU.mult, ALU.mult, [bxin, bvec, brs], [bcn])

    def rope(row_x, row_sw, t0, n, out, bout):
        kb.load(kp[0:64, 0:n], P1[row_x:row_x + 64, 1 + t0:1 + t0 + n], [bkp], r=[bP1])
        kb.load(kps[0:64, 0:n], P1[row_sw:row_sw + 64, 1 + t0:1 + t0 + n], [bkps], r=[bP1])
        kb.load(rc[0:64, 0:n], dram["ropeC"][:, t0:t0 + n], [brc])
        kb.load(rsn[0:64, 0:n], dram["ropeS"][:, t0:t0 + n], [brsn])
        kb.tt(t1[0:64, 0:n], kp[0:64, 0:n], rc[0:64, 0:n], ALU.mult, [bkp, brc], [bt1])
        kb.tt(t2[0:64, 0:n], kps[0:64, 0:n], rsn[0:64, 0:n], ALU.mult, [bkps, brsn], [bt2], eng="pool")
        kb.tt(out, t1[0:64, 0:n], t2[0:64, 0:n], ALU.add, [bt1, bt2], [bout])

    ei = 0
    for (t0, n) in [(0, 256)] + tok_blocks(256, T, 512):
        rms4(C_CKV, t0, n, "kvn")
        for h in range(4):
            pp, bpp = kb.ps()
            for c in range(4):
                kb.mm(pp[:, 0:n], wkk[:, c, h * 128:(h + 1) * 128], cn[:, c, 0:n], c == 0, c == 3, [bwkk, bcn], bpp)
            kb.copy(Kn[:, h, t0:t0 + n], pp[:, 0:n], [bpp], [bKn], eng=("act" if ei % 2 else "dve")); ei += 1
        for tt_ in range(n // 128):
            pp, bpp = kb.ps()
            for c in range(4):
                kb.mm(pp[:, 0:512], cn[:, c, tt_ * 128:(tt_ + 1) * 128], wkv[:, c, :], c == 0, c == 3, [bwkv, bcn], bpp)
            kb.copy(Vt[:, (t0 // 128) + tt_, :], pp[:, 0:512], [bpp], [bVt], eng=("act" if ei % 2 else "dve")); ei += 1
        rope(C_KPE, C_KPESW, t0, n, Kr[0:64, t0:t0 + n], bKr)
    Qn, bQn = T_("Qn", [4, 512], BF16); Qr, bQr = T_("Qr", [4, 512], BF16)
    kb.memset(Qr[64:128, :, :], 0.0, [bQr])
    qa, bqa = T_("qa", [512]); qb, bqb = T_("qb", [512])
    pts = [T_(f"pt{i}", [512], BF16) for i in range(3)]
    rden, brden = T_("rden", [512]); ob, bob = T_("ob", [512], BF16)
    pti = 0
    for qi in range(8):
        t0 = TCX + qi * 512
        rms4(C_CQ, t0, 512, "qn")
        kb.load(rc[0:64, :], dram["ropeC"][:, t0:t0 + 512], [brc])
        kb.load(rsn[0:64, :], dram["ropeS"][:, t0:t0 + 512], [brsn])
        for h in range(4):
            pp, bpp = kb.ps()
            for c in range(4):
                kb.mm(pp[:, :], wq[:, c, h * 256:h * 256 + 128], cn[:, c, :], c == 0, c == 3, [bwq, bcn], bpp)
            kb.copy(Qn[:, h, :], pp[:, :], [bpp], [bQn], eng="act")
            p1_, bp1_ = kb.ps(); p2_, bp2_ = kb.ps()
            for c in range(4):
                kb.mm(p1_[0:64, :], wq[:, c, h * 256 + 128:h * 256 + 192], cn[:, c, :], c == 0, c == 3, [bwq, bcn], bp1_)
            for c in range(4):
                kb.mm(p2_[0:64, :], wq[:, c, h * 256 + 192:h * 256 + 256], cn[:, c, :], c == 0, c == 3, [bwq, bcn], bp2_)
            kb.tt(qa[0:64, :], p1_[0:64, :], rc[0:64, :], ALU.mult, [bp1_, brc], [bqa])
            kb.tt(qb[0:64, :], p2_[0:64, :], rsn[0:64, :], ALU.mult, [bp2_, brsn], [bqb])
            kb.tt(Qr[0:64, h, :], qa[0:64, :], qb[0:64, :], ALU.add, [bqa, bqb], [bQr], eng="pool")
        for h in range(4):
            po, bpo = kb.ps(hold=True); pd, bpd = kb.ps(hold=True)
            for kt in range(34):
                pS, bpS = kb.ps()
                kb.mm(pS[:, :], Kn[:, h, kt * 128:(kt + 1) * 128], Qn[:, h, :], True, False, [bKn, bQn], bpS)
                kb.mm(pS[:, :], Kr[:, kt * 128:(kt + 1) * 128], Qr[:, h, :], False, True, [bKr, bQr], bpS)
                pt, bpt = pts[pti % 3]; pti += 1
                kb.act(pt, pS[:, :], AF.Exp, [bpS], [bpt], scale=ATTN_SCALE)
                kb.mm(po[:, :], Vt[:, kt, h * 128:(h + 1) * 128], pt, kt == 0, kt == 33, [bVt, bpt], bpo)
                kb.mm(pd[:, :], ones_bf, pt, kt == 0, kt == 33, [bones, bpt], bpd)
            kb.recip(rden, pd[:, :], [bpd], [brden])
            kb.tt(ob, po[:, :], rden, ALU.mult, [bpo, brden], [bob])
            kb.unhold(bpo); kb.unhold(bpd)
            kb.store(XBs[2 + h][:, qi * 512:(qi + 1) * 512], ob, [bob], [bXBs[2 + h]])
    A.release(m_phase)

def phase_exchange(cx, chunks):
    P = cx["P"]
    XBs, bXBs, XBd, bXBd = cx["XBs"], cx["bXBs"], cx["XBd"], cx["bXBd"]
    for c in chunks:
        if os.environ.get("SKIP_CC") == "1":
            continue
        def fn(e, c=c):
            return e.collective_compute("AllGather", ALU.bypass, replica_groups=[[0, 1, 2, 3], [4, 5, 6, 7]], ins=[XBs[c]], outs=[XBd[c]])
        P.special("pool", f"cc{c}", fn, reads=[bXBs[c]], writes=[bXBd[c]])


def phase56(cx):
    kb, P, A, V, vec, bvec = cx["kb"], cx["P"], cx["A"], cx["V"], cx["vec"], cx["bvec"]
    XBd, bXBd, X1, bX1, oT, bOT = cx["XBd"], cx["bXBd"], cx["X1"], cx["bX1"], cx["oT"], cx["bOT"]
    ones_bf, bones = cx["ones_bf"], cx["bones"]
    DV, der, bder = cx["DV"], cx["der"], cx["bder"]
    norm_mod = cx["norm_mod"]
    A1, B1, A1C, B1C, A2, B2, G1, G2 = cx["consts"]
    dram, voff = cx["dram"], cx["voff"]
    TB = [(0, 342), (342, 342), (684, 342)]
    xown3 = dram["xT_own"].rearrange("(c p) t -> p c t", p=128)

    def T_(name, shape, dt=F32):
        return A.alloc2(name, shape, dt)
    m_phase = A.mark()
    ws = WStream(kb, nst=3)
    m5 = A.mark()
    hTo, bhTo = T_("hTo", [16, NHALO], BF16)
    Yf, bYf = T_("Yf", [8, NHALO], BF16); Of, bOf = T_("Of", [16, NHALO], BF16)
    m = A.mark()
    xblk, bxblk = T_("xblk", [16, 342]); sqb, bsq = T_("sqb5", [16, 342], BF16); rs, brs = T_("rs5", [342])
    for (t0, n) in TB:
        kb.load(xblk[:, :, 0:n], xown3[:, :, t0:t0 + n], [bxblk])
        norm_mod(xblk, bxblk, n, hTo[:, :, t0:t0 + n], bhTo, sqb, bsq, rs, brs, A1, B1)
    A.release(m)
    m = A.mark()
    lds = [T_(f"ld{i}", [NHALO], BF16) for i in range(3)]
    li = 0
    osel, _ = voff["sel"]
    for r in range(4):
        for fc in range(6):
            dst = Yf[:, r * 2 + fc, :] if fc < 2 else Of[:, r * 4 + (fc - 2), :]
            bdst = bYf if fc < 2 else bOf
            for j in range(4):
                ld, bld = lds[li % 3]; li += 1
                c0 = max(0, 1024 * j - 1); c1 = min(T, 1024 * j + NHALO - 1)
                o0 = c0 - (1024 * j - 1)
                if j == 0:
                    kb.memset(ld[:, 0:1], 0.0, [bld])
                if j == 3:
                    kb.memset(ld[:, NHALO - 1:NHALO], 0.0, [bld])
                kb.load(ld[:, o0:o0 + (c1 - c0)], XBd[fc][r * 128:(r + 1) * 128, c0:c1], [bld], r=[bXBd[fc]])
                if j == 0:
                    kb.ts(dst, ld, vec[:, osel:osel + 1], None, ALU.mult, None, [bld, bvec], [bdst])
                else:
                    kb.stt(dst, ld, vec[:, osel + j:osel + j + 1], dst, ALU.mult, ALU.add, [bld, bvec, bdst], [bdst])
    A.release(m)
    mrg, bmrg = T_("mrg", [16, NHALO], BF16)
    m = A.mark()
    M1, bM1 = T_("M1", [NHALO]); M2, bM2 = T_("M2", [NHALO]); GA, bGA = T_("GA", [NHALO])
    ogb, _ = voff["gate_b"]
    _reqs = []
    for jc in range(16):
        _reqs += [(dram["w_rp"], 8, jc), (dram["w_gate_in"], 16, jc),
                  (dram["w_mp"], 16, jc), (dram["w_gate_in"], 16, 16 + jc)]
    _wg = ws.stream(_reqs)
    for jc in range(16):
        def prod(W, kc, col0, rhs_t, brhs, outt, bout, sig_bias=None):
            wbf, bw = next(_wg)
            for (t0, n) in TB:
                pp, bpp = kb.ps()
                for c in range(kc):
                    kb.mm(pp[:, 0:n], wbf[:, c, :], rhs_t[:, c, t0:t0 + n], c == 0, c == kc - 1, [bw, brhs], bpp)
                if sig_bias is None:
                    kb.copy(outt[:, t0:t0 + n], pp[:, 0:n], [bpp], [bout], eng="act")
                else:
                    kb.act(outt[:, t0:t0 + n], pp[:, 0:n], AF.Sigmoid, [bpp, bvec], [bout], bias=sig_bias)
        prod(dram["w_rp"], 8, jc * 128, Yf, bYf, M1, bM1)
        prod(dram["w_gate_in"], 16, jc * 128, hTo, bhTo, GA, bGA, sig_bias=vec[:, ogb + jc:ogb + jc + 1])
        kb.tt(M1, M1, GA, ALU.mult, [bM1, bGA], [bM1])
        prod(dram["w_mp"], 16, jc * 128, Of, bOf, M2, bM2)
        prod(dram["w_gate_in"], 16, 2048 + jc * 128, hTo, bhTo, GA, bGA, sig_bias=vec[:, ogb + 16 + jc:ogb + 16 + jc + 1])
        kb.tt(M2, M2, GA, ALU.mult, [bM2, bGA], [bM2], eng="pool")
        kb.tt(mrg[:, jc, :], M1, M2, ALU.add, [bM1, bM2], [bmrg])
    A.release(m)
    A.release(m5)
    mrg2, bmrg2 = T_("mrg", [16, NHALO], BF16)
    kb.copy(mrg2, mrg, [bmrg], [bmrg2], eng="pool")
    m5b = A.mark()
    outT, boutT = T_("outT", [16, NHALO])
    for jc, (wbf, bw) in enumerate(ws.stream([(dram["w_out"], 16, jc) for jc in range(16)])):
        for (t0, n) in TB:
            pp, bpp = kb.ps()
            for c in range(16):
                kb.mm(pp[:, 0:n], wbf[:, c, :], mrg2[:, c, t0:t0 + n], c == 0, c == 15, [bw, bmrg2], bpp)
            kb.copy(outT[:, jc, t0:t0 + n], pp[:, 0:n], [bpp], [boutT], eng=("act" if jc % 2 else "dve"))
    m = A.mark()
    xblk, bxblk = T_("xblk", [16, 342]); sqb, bsq = T_("sqb5", [16, 342], BF16); rs, brs = T_("rs5", [342])
    hT2, bhT2 = mrg2, bmrg2
    for (t0, n) in TB:
        kb.act(sqb[:, :, 0:n], outT[:, :, t0:t0 + n], AF.Square, [boutT], [bsq])
        pp, bpp = kb.ps()
        for c in range(16):
            kb.mm(pp[:, 0:n], ones_bf, sqb[:, c, 0:n], c == 0, c == 15, [bones, bsq], bpp)
        kb.rsqrt(rs[:, 0:n], pp[:, 0:n], 1.0 / D, EPS, [bpp], brs)
        kb.load(xblk[:, :, 0:n], xown3[:, :, t0:t0 + n], [bxblk])
        for c in range(16):
            kb.stt(outT[:, c, t0:t0 + n], outT[:, c, t0:t0 + n], DV(G1, c), rs[:, 0:n], ALU.mult, ALU.mult, [boutT, bder, brs], [boutT])
        kb.tt(outT[:, :, t0:t0 + n], outT[:, :, t0:t0 + n], xblk[:, :, 0:n], ALU.add, [boutT, bxblk], [boutT], eng="pool")
        kb.store(X1.rearrange("(c p) t -> p c t", p=128)[:, :, t0:t0 + n], outT[:, :, t0:t0 + n], [boutT], [bX1])
        kb.copy(xblk[:, :, 0:n], outT[:, :, t0:t0 + n], [boutT], [bxblk], eng="pool")
        norm_mod(xblk, bxblk, n, hT2[:, :, t0:t0 + n], bhT2, sqb, bsq, rs, brs, A2, B2)
    A.release(m5b)
    h2, bh2 = hT2, bhT2
    yacc, byacc = T_("yacc", [16, NOWN]); m6 = A.mark(); agrp, bagrp = T_("agrp", [11, NOWN], BF16)
    upre, bup = T_("upre", [NHALO]); u, bu = T_("u", [NOWN]); ge, bge = T_("ge", [NOWN])
    ocw, _ = voff["conv_w"]; ocb, _ = voff["conv_b"]; ohm, _ = voff["hmask"]
    OB = [(0, 512), (512, 512)]
    _reqs = []
    for grp in range(4):
        for f in range(11):
            _reqs += [(dram["w_fg"], 16, grp * 11 + f), (dram["w_fv"], 16, grp * 11 + f)]
        _reqs += [(dram["w_fd"], 11, grp * 16 + nc_) for nc_ in range(16)]
    _wg = ws.stream(_reqs)
    for grp in range(4):
        for f in range(11):
            ff = grp * 11 + f
            wbf, bw = next(_wg)
            for (t0, n) in TB:
                pp, bpp = kb.ps()
                for c in range(16):
                    kb.mm(pp[:, 0:n], wbf[:, c, :], h2[:, c, t0:t0 + n], c == 0, c == 15, [bw, bh2], bpp)
                kb.copy(upre[:, t0:t0 + n], pp[:, 0:n], [bpp], [bup], eng="act")
            kb.ts(upre[:, 0:1], upre[:, 0:1], vec[:, ohm:ohm + 1], None, ALU.mult, None, [bup, bvec], [bup])
            kb.ts(upre[:, NHALO - 1:NHALO], upre[:, NHALO - 1:NHALO], vec[:, ohm + 1:ohm + 2], None, ALU.mult, None, [bup, bvec], [bup])
            kb.ts(u, upre[:, 0:NOWN], vec[:, ocw + ff:ocw + ff + 1], vec[:, ocb + ff:ocb + ff + 1], ALU.mult, ALU.add, [bup, bvec], [bu])
            kb.stt(u, upre[:, 1:NOWN + 1], vec[:, ocw + 44 + ff:ocw + 44 + ff + 1], u, ALU.mult, ALU.add, [bup, bvec, bu], [bu])
            kb.stt(u, upre[:, 2:NOWN + 2], vec[:, ocw + 88 + ff:ocw + 88 + ff + 1], u, ALU.mult, ALU.add, [bup, bvec, bu], [bu])
            kb.act(ge, u, AF.Gelu_apprx_tanh, [bu], [bge])
            wbf, bw = next(_wg)
            for (t0, n) in OB:
                pp, bpp = kb.ps()
                for c in range(16):
                    kb.mm(pp[:, 0:n], wbf[:, c, :], h2[:, c, 1 + t0:1 + t0 + n], c == 0, c == 15, [bw, bh2], bpp)
                kb.tt(agrp[:, f, t0:t0 + n], pp[:, 0:n], ge[:, t0:t0 + n], ALU.mult, [bpp, bge], [bagrp])
        for nc_ in range(16):
            wbf, bw = next(_wg)
            for (t0, n) in OB:
                pp, bpp = kb.ps()
                for f in range(11):
                    kb.mm(pp[:, 0:n], wbf[:, f, :], agrp[:, f, t0:t0 + n], f == 0, f == 10, [bw, bagrp], bpp)
                if grp == 0:
                    kb.copy(yacc[:, nc_, t0:t0 + n], pp[:, 0:n], [bpp], [byacc], eng="act")
                else:
                    kb.tt(yacc[:, nc_, t0:t0 + n], yacc[:, nc_, t0:t0 + n], pp[:, 0:n], ALU.add, [byacc, bpp], [byacc])
    A.release(m6)
    yacc2, byacc2 = yacc, byacc
    x1b, bx1b = T_("x1b", [16, 256]); sq6, bsq6 = T_("sq6", [16, 256], BF16); rs6, brs6 = T_("rs6", [256])
    X13 = X1.rearrange("(c p) t -> p c t", p=128)
    oT3 = oT.rearrange("(c p) t -> p c t", p=128)
    for (t0, n) in [(0, 256), (256, 256), (512, 256), (768, 256)]:
        kb.act(sq6, yacc2[:, :, t0:t0 + n], AF.Square, [byacc2], [bsq6])
        pp, bpp = kb.ps()
        for c in range(16):
            kb.mm(pp[:, 0:n], ones_bf, sq6[:, c, :], c == 0, c == 15, [bones, bsq6], bpp)
        kb.rsqrt(rs6, pp[:, 0:n], 1.0 / D, EPS, [bpp], brs6)
        kb.load(x1b, X13[:, :, 1 + t0:1 + t0 + n], [bx1b], r=[bX1])
        for c in range(16):
            kb.stt(yacc2[:, c, t0:t0 + n], yacc2[:, c, t0:t0 + n], DV(G2, c), rs6, ALU.mult, ALU.mult, [byacc2, bder, brs6], [byacc2])
        kb.tt(x1b, x1b, yacc2[:, :, t0:t0 + n], ALU.add, [bx1b, byacc2], [bx1b], eng="pool")
        kb.store(oT3[:, :, t0:t0 + n], x1b, [bx1b], [bOT])
    A.release(m_phase)

_ROPE_PARTNER = np.array([(j + 16) if (j % 32) < 16 else (j - 16) for j in range(64)])


def _rope_tables():
    half = 32
    freqs = (np.float32(10000.0) ** (-np.arange(0, half, 2, dtype=np.float32) / np.float32(half))).astype(np.float32)
    tpos = np.arange(T)
    row = (tpos // 64).astype(np.float32); col = (tpos % 64).astype(np.float32)
    ar = row[:, None] * freqs; ac = col[:, None] * freqs
    C = np.ones((64, TALL), np.float32); S = np.zeros((64, TALL), np.float32)
    for j in range(64):
        ang = ar[:, j % 16] if j < 32 else ac[:, j % 16]
        C[j, TCX:] = np.cos(ang)
        S[j, TCX:] = (-np.sin(ang)) if (j % 32) < 16 else np.sin(ang)
    return C, S


def _masks():
    r = (np.arange(128) % 64)[:, None]; c = (np.arange(512) % 64)[None, :]
    M = np.zeros((128, 7 * 512), np.float32)
    M[:, 0:512] = (r < c); M[:, 512:1024] = (r > c); M[:, 1024:1536] = (r <= c); M[:, 1536:2048] = (r >= c); M[:, 2048:2560] = (r == c)
    M[:, 2560:2688] = ((np.arange(128) // 64)[:, None] == (np.arange(128) // 64)[None, :])
    return M


def _blk(W, kc):
    K, M = W.shape
    assert K == kc * 128 and M % 128 == 0
    return np.ascontiguousarray(W.reshape(kc, 128, M // 128, 128).transpose(2, 1, 0, 3).reshape(M // 128, 128, kc * 128))


def _prep_core(inp, b, g, shared):
    f32 = np.float32
    vp = VecPack()
    cT = np.stack([fm(inp["c"][b]), fm(inp["c_ctx"])], axis=2)
    vp.add("cT", cT.reshape(128, 32))
    vp.add("b_ada", fm(inp["b_ada"][0]))
    vp.add("npre", fm(inp["norm_mix_pre"][0])); vp.add("npost", fm(inp["norm_mix_post"][0]))
    vp.add("nfpre", fm(inp["norm_ffn_pre"][0])); vp.add("nfpost", fm(inp["norm_ffn_post"][0]))
    vp.add("gate_b", fm(inp["gate_b"][0]))
    vp.add("conv_w", np.concatenate([fm(inp["ffn_conv_w"][0][j]) for j in range(3)], axis=1))
    vp.add("conv_b", fm(inp["ffn_conv_b"][0]))
    vp.add("hmask", np.tile(np.array([[1.0 if g > 0 else 0.0, 1.0 if g < 3 else 0.0]], f32), (128, 1)))
    sel = np.zeros((128, 4), f32); sel[:, g] = 1.0
    vp.add("sel", sel)
    mu = inp["rwkv_mu"][0]
    ch0 = 256 * g

    def pad96(v):
        o = np.zeros(128, f32); o[:96] = v
        return o
    mus = []
    for base in (0, 1024, 2048):
        for hp in range(2):
            mus.append(mu[base + ch0 + hp * 128: base + ch0 + (hp + 1) * 128])
    for base in (3072, 3264):
        for d in range(2):
            mus.append(pad96(mu[base + d * 96: base + (d + 1) * 96]))
    for c in range(2):
        mus.append(mu[3456 + c * 128:3456 + (c + 1) * 128])
    vp.add("mus", np.stack(mus, axis=1))

    def own2(v):
        return np.stack([v[ch0 + hp * 128: ch0 + (hp + 1) * 128] for hp in range(2)], axis=1)
    vp.add("w0", np.stack([inp["rwkv_w0"][0][d][ch0 + hp * 128: ch0 + (hp + 1) * 128] for d in range(2) for hp in range(2)], axis=1))
    vp.add("a0", np.stack([inp["rwkv_a0"][0][d][ch0 + hp * 128: ch0 + (hp + 1) * 128] for d in range(2) for hp in range(2)], axis=1))
    vp.add("k_k", own2(inp["rwkv_k_k"][0])); vp.add("k_a", own2(inp["rwkv_k_a"][0]))
    vp.add("r_k", own2(inp["rwkv_r_k"][0].reshape(-1)))
    vp.add("lnx_w", own2(inp["rwkv_lnx_w"][0])); vp.add("lnx_b", own2(inp["rwkv_lnx_b"][0]))
    vp.add("qn", fm(inp["mla_q_norm"][0])); vp.add("kvn", fm(inp["mla_kv_norm"][0]))
    w_in = inp["w_in"][0]
    cols = np.concatenate([np.arange(ch0, ch0 + 256), 1024 + np.arange(ch0, ch0 + 256), 2048 + np.arange(ch0, ch0 + 256),
                           np.arange(3072, 3712), np.arange(3712, 4800), 4736 + _ROPE_PARTNER])
    assert cols.shape[0] == 2560
    heads = range(4 * g, 4 * g + 4)
    uq_cols = np.concatenate([np.concatenate([h * 192 + np.arange(192), h * 192 + 128 + _ROPE_PARTNER]) for h in heads])
    ukvk = np.concatenate([h * 256 + np.arange(128) for h in heads]); ukvv = np.concatenate([h * 256 + 128 + np.arange(128) for h in heads])
    x_b = inp["x"][b]
    idx = np.clip(1024 * g - 1 + np.arange(NHALO), 0, T - 1)
    m = dict(shared)
    m.update({
        "xT_all": shared["_xT_all"][b], "xT_own": np.ascontiguousarray(x_b[idx].T),
        "vec": vp.build(),
        "w_inA": _blk(np.ascontiguousarray(w_in[:, cols]), 16),
        "w2": np.ascontiguousarray(inp["rwkv_w2"][0][:, :, ch0:ch0 + 256]), "a2": np.ascontiguousarray(inp["rwkv_a2"][0][:, :, ch0:ch0 + 256]),
        "g2": np.ascontiguousarray(inp["rwkv_g2"][0][:, ch0:ch0 + 256]),
        "w_uq": np.ascontiguousarray(inp["mla_w_uq"][0][:, uq_cols]),
        "w_ukvk": np.ascontiguousarray(inp["mla_w_ukv"][0][:, ukvk]), "w_ukvv": np.ascontiguousarray(inp["mla_w_ukv"][0][:, ukvv]),
    })
    for k in [k for k in m if k.startswith("_")]:
        del m[k]
    return m, vp


_CACHE = {}


def _get_program(voff, nvec, upto=99, dump=None):
    key = ("nc", upto, dump)
    if key not in _CACHE:
        nc = bass.Bass("TRN2", target_bir_lowering=False)
        cx = build_program(nc, voff, nvec)
        if upto >= 3:
            try:
                phase3_rwkv(cx)
            except _Stop:
                pass
        if upto >= 5:
            phase_exchange(cx, [0, 1])
        if upto >= 4:
            phase4_mla(cx)
        if upto >= 5:
            phase_exchange(cx, [2, 3, 4, 5])
        if upto >= 6:
            phase56(cx)
        if dump is None:
            cx["P"].final_wait("sp", [cx["bOT"]])
        else:
            dn = dump.split(":")
            if dn[0] == "der":
                dbg = nc.dram_tensor("dbg", [128, 160], F32, kind="ExternalOutput").ap()
                bdbg = Buf("dbg")
                cx["kb"].store(dbg, cx["der"], [cx["bder"]], [bdbg])
                cx["P"].final_wait("sp", [bdbg])
                cx["P"].emit(cx["st"]); cx["st"].close()
                _CACHE[key] = nc
                _CACHE["stats"] = (cx["P"].nops, cx["A"].peak, {k: len(v) for k, v in cx["P"].ops.items()})
                return nc
            src_ap, src_buf = cx[dn[0]], cx["b" + dn[0]]
            if len(dn) == 3:
                src_ap = src_ap[int(dn[1]):int(dn[2])]
            dt = BF16 if dn[0] in ("XBs", "XBd") else F32
            dbg = nc.dram_tensor("dbg", list(src_ap.shape), dt, kind="ExternalOutput").ap()
            bdbg = Buf("dbg")
            cx["kb"].store(dbg, src_ap, [src_buf], [bdbg])
            cx["P"].final_wait("sp", [bdbg])
        cx["P"].emit(cx["st"])
        cx["st"].close()
        _CACHE[key] = nc
        _CACHE["stats"] = (cx["P"].nops, cx["A"].peak, {k: len(v) for k, v in cx["P"].ops.items()})
    return _CACHE[key]


def kernel(**inputs):
    inp = {k: np.asarray(v, dtype=np.float32) for k, v in inputs.items()}
    C, S = _rope_tables()
    shared = {
        "w_ada": np.ascontiguousarray(inp["w_ada"][0]), "w_gate_in": _blk(np.ascontiguousarray(inp["w_in"][0][:, 4800:8896]), 16),
        "w_rp": _blk(inp["w_rwkv_proj"][0], 8), "w_mp": _blk(inp["w_mla_proj"][0], 16),
        "w_out": _blk(inp["w_out"][0], 16), "w_fg": _blk(inp["ffn_w_gate"][0], 16),
        "w_fv": _blk(inp["ffn_w_val"][0], 16),
        "w_fd": np.concatenate([_blk(inp["ffn_w_down"][0][gq * 1408:(gq + 1) * 1408], 11) for gq in range(4)], axis=0),
        "ident": np.eye(128, dtype=np.float32), "masks": _masks(), "ropeC": C, "ropeS": S,
        "_xT_all": [np.ascontiguousarray(np.concatenate([inp["ctx"][b], inp["x"][b]], axis=0).T) for b in range(2)],
    }
    in_maps = []
    vp = None
    for core in range(8):
        m, vp = _prep_core(inp, core // 4, core % 4, shared)
        in_maps.append(m)
    nc = _get_program(vp.off, vp.n)
    res = run_bass_kernel_spmd(nc, in_maps, core_ids=list(range(8)))
    out = np.empty((2, T, D), np.float32)
    for core in range(8):
        b, g = core // 4, core % 4
        out[b, 1024 * g:1024 * (g + 1), :] = res.results[core]["oT"].T
    return out
```
